# Optimizing a Trainium2 kernel written in Bass

```python
import math
import jax
import jax.numpy as jnp
from jax import lax
import numpy as np

D_MODEL = 2048
BATCH = 2
SEQ = 4096
DEPTH = 4

HEAD_DIM = 128
N_MIX_HEADS = D_MODEL // HEAD_DIM
A_Q_HEADS = N_MIX_HEADS // 4
A_KV_HEADS = max(1, A_Q_HEADS // 2)
A_GROUP = A_Q_HEADS // A_KV_HEADS
A_RADIUS = 128
B_PAIRS = ((128, 1), (512, 4), (2048, 16))
B_N_GROUPS = len(B_PAIRS)
B_HEADS_PER_GROUP = (N_MIX_HEADS - A_Q_HEADS) // B_N_GROUPS
B_KV_HEADS = B_HEADS_PER_GROUP
AB_IN_WIDTH = HEAD_DIM * (A_Q_HEADS + 2 * A_KV_HEADS + B_N_GROUPS * B_HEADS_PER_GROUP + 2 * B_KV_HEADS)
AB_OUT_WIDTH = HEAD_DIM * (A_Q_HEADS + B_HEADS_PER_GROUP)
C_HEADS = D_MODEL // (2 * HEAD_DIM)
C_IN_WIDTH = 3 * D_MODEL
D_FF = (D_MODEL * 43 // 16 + 127) // 128 * 128
CONV_WIDTH = 3
QBLOCK = 128
EPS = 1e-6
NEG_INF = -1e30
N_EVEN = (DEPTH + 1) // 2
N_ODD = DEPTH // 2

kernel_name = "hybrid_local_dilated_diff_encoder"


def rms_norm(x, g):
    xf = x.astype(jnp.float32)
    y = xf * lax.rsqrt(jnp.mean(xf * xf, axis=-1, keepdims=True) + EPS)
    return (y * g.astype(jnp.float32)).astype(x.dtype)


def alibi_slopes(n):
    return 2.0 ** (-8.0 * jnp.arange(1, n + 1, dtype=jnp.float32) / n)


def banded_attention(q, k, v, radius, slopes, step, sink=None):
    z, length, hk, g, dh = q.shape
    nb = -(-length // radius)
    pad = nb * radius - length
    qb = jnp.pad(q, ((0, 0), (0, pad), (0, 0), (0, 0), (0, 0))).reshape(z, nb, radius, hk, g, dh)
    kv_pad = ((0, 0), (radius, radius + pad), (0, 0), (0, 0))

    def windows(t):
        tb = jnp.pad(t, kv_pad).reshape(z, nb + 2, radius, hk, dh)
        return jnp.concatenate([tb[:, :-2], tb[:, 1:-1], tb[:, 2:]], axis=2)

    kw, vw = windows(k), windows(v)
    s = jnp.einsum('znqhgd,znkhd->znhgqk', qb, kw).astype(jnp.float32) * (dh ** -0.5)
    qpos = jnp.arange(nb * radius).reshape(nb, radius)
    kpos = jnp.arange(nb)[:, None] * radius - radius + jnp.arange(3 * radius)[None, :]
    dist = jnp.abs(kpos[:, None, :] - qpos[:, :, None])
    valid = (dist <= radius) & (kpos >= 0)[:, None, :] & (kpos < length)[:, None, :]
    bias = -(slopes.astype(jnp.float32)[None, :, :, None, None]
             * (step * dist).astype(jnp.float32)[:, None, None])
    s = jnp.where(valid[:, None, None], s + bias, NEG_INF)
    m = s.max(axis=-1)
    if sink is not None:
        sk = sink.astype(jnp.float32)[None, None, :, :, None]
        m = jnp.maximum(m, sk)
    p = jnp.exp(s - m[..., None])
    den = p.sum(axis=-1)
    if sink is not None:
        den = den + jnp.exp(sk - m)
    o = jnp.einsum('znhgqk,znkhd->znqhgd', p.astype(v.dtype), vw).astype(jnp.float32)
    den_t = jnp.transpose(den, (0, 1, 4, 2, 3))
    lse_t = jnp.transpose(m + jnp.log(den), (0, 1, 4, 2, 3))
    o = (o / den_t[..., None]).reshape(z, nb * radius, hk, g, dh)[:, :length]
    lse = lse_t.reshape(z, nb * radius, hk, g)[:, :length]
    return o.astype(q.dtype), lse


def fold_stride(t, d):
    b, s = t.shape[0], t.shape[1]
    rest = t.shape[2:]
    t = jnp.moveaxis(t.reshape((b, s // d, d) + rest), 2, 1)
    return t.reshape((b * d, s // d) + rest)


def unfold_stride(t, d, b):
    l = t.shape[1]
    rest = t.shape[2:]
    t = jnp.moveaxis(t.reshape((b, d, l) + rest), 1, 2)
    return t.reshape((b, l * d) + rest)


def local_dilated_mixer(h, w_in, w_out, sink):
    b, s, _ = h.shape
    proj = h @ w_in
    sizes = [A_Q_HEADS * HEAD_DIM, A_KV_HEADS * HEAD_DIM, A_KV_HEADS * HEAD_DIM,
             B_N_GROUPS * B_HEADS_PER_GROUP * HEAD_DIM, B_KV_HEADS * HEAD_DIM, B_KV_HEADS * HEAD_DIM]
    idx = [sum(sizes[:i + 1]) for i in range(len(sizes) - 1)]
    qa, ka, va, qb, kb, vb = jnp.split(proj, idx, axis=-1)
    slopes = alibi_slopes(N_MIX_HEADS)

    qa = qa.reshape(b, s, A_KV_HEADS, A_GROUP, HEAD_DIM)
    ka = ka.reshape(b, s, A_KV_HEADS, HEAD_DIM)
    va = va.reshape(b, s, A_KV_HEADS, HEAD_DIM)
    oa, _ = banded_attention(qa, ka, va, A_RADIUS, slopes[:A_Q_HEADS].reshape(A_KV_HEADS, A_GROUP), 1,
                             sink=sink.reshape(A_KV_HEADS, A_GROUP))
    oa = oa.reshape(b, s, A_Q_HEADS * HEAD_DIM)

    qb = qb.reshape(b, s, B_N_GROUPS, B_HEADS_PER_GROUP, HEAD_DIM)
    kb = kb.reshape(b, s, B_KV_HEADS, HEAD_DIM)
    vb = vb.reshape(b, s, B_KV_HEADS, HEAD_DIM)
    outs, lses = [], []
    for gi, (window, dil) in enumerate(B_PAIRS):
        radius = window // (2 * dil)
        start = A_Q_HEADS + gi * B_HEADS_PER_GROUP
        sl = slopes[start:start + B_HEADS_PER_GROUP][:, None]
        o, lse = banded_attention(fold_stride(qb[:, :, gi], dil)[:, :, :, None, :],
                                  fold_stride(kb, dil), fold_stride(vb, dil), radius, sl, dil)
        outs.append(unfold_stride(o[:, :, :, 0], dil, b))
        lses.append(unfold_stride(lse[..., 0], dil, b))
    wts = jax.nn.softmax(jnp.stack(lses, axis=0), axis=0)
    ob = jnp.sum(wts[..., None] * jnp.stack(outs, axis=0).astype(jnp.float32), axis=0)
    ob = ob.astype(h.dtype).reshape(b, s, B_HEADS_PER_GROUP * HEAD_DIM)
    return jnp.concatenate([oa, ob], axis=-1) @ w_out


def diff_attention(h, w_in, w_out, lam_params, subln_g, lambda_init):
    b, s, _ = h.shape
    q, k, v = jnp.split(h @ w_in, 3, axis=-1)
    q = q.reshape(b, s, C_HEADS, 2, HEAD_DIM)
    k = k.reshape(b, s, C_HEADS, 2, HEAD_DIM)
    v = v.reshape(b, s, C_HEADS, 2 * HEAD_DIM)
    lp = lam_params.astype(jnp.float32)
    lam = jnp.exp(jnp.sum(lp[0] * lp[1])) - jnp.exp(jnp.sum(lp[2] * lp[3])) + lambda_init
    slopes = alibi_slopes(C_HEADS)
    nb = s // QBLOCK
    qblocks = jnp.moveaxis(q.reshape(b, nb, QBLOCK, C_HEADS, 2, HEAD_DIM), 1, 0)
    kpos = jnp.arange(s)
    scale = HEAD_DIM ** -0.5

    def block(args):
        qblk, start = args
        sc = jnp.einsum('bqhjd,bkhjd->bhjqk', qblk, k).astype(jnp.float32) * scale
        qpos = start + jnp.arange(QBLOCK)
        dist = jnp.abs(qpos[:, None] - kpos[None, :]).astype(jnp.float32)
        sc = sc - slopes[None, :, None, None, None] * dist
        a = jax.nn.softmax(sc, axis=-1)
        attn = a[:, :, 0] - lam * a[:, :, 1]
        return jnp.einsum('bhqk,bkhe->bqhe', attn.astype(v.dtype), v)

    o = lax.map(block, (qblocks, jnp.arange(nb) * QBLOCK))
    o = jnp.moveaxis(o, 0, 1).reshape(b, s, C_HEADS, 2 * HEAD_DIM)
    o = rms_norm(o, subln_g) * (1.0 - lambda_init)
    return o.reshape(b, s, D_MODEL) @ w_out


def conv_ffn(h, w_up, conv_w, conv_b, w_down):
    gate, up = jnp.split(h @ w_up, 2, axis=-1)
    gp = jnp.pad(gate, ((0, 0), (1, 1), (0, 0)))
    gate = gp[:, :-2] * conv_w[0] + gp[:, 1:-1] * conv_w[1] + gp[:, 2:] * conv_w[2] + conv_b
    return (jax.nn.gelu(gate, approximate=True) * up) @ w_down


def setup_inputs(seed: int = 0) -> dict:
    key = jax.random.key(seed)
    ks = jax.random.split(key, 16)

    def nrm(k, shape, scale):
        return jax.random.normal(k, shape, jnp.float32) * scale

    return {
        "x": nrm(ks[0], (BATCH, SEQ, D_MODEL), 1.0),
        "c": nrm(ks[1], (BATCH, D_MODEL), 1.0),
        "ada_w": nrm(ks[2], (DEPTH, D_MODEL, 6 * D_MODEL), D_MODEL ** -0.5),
        "ada_b": nrm(ks[3], (DEPTH, 6 * D_MODEL), 0.02),
        "norm_g": 1.0 + nrm(ks[4], (DEPTH, 4, D_MODEL), 0.02),
        "ab_w_in": nrm(ks[5], (N_EVEN, D_MODEL, AB_IN_WIDTH), D_MODEL ** -0.5),
        "ab_w_out": nrm(ks[6], (N_EVEN, AB_OUT_WIDTH, D_MODEL), AB_OUT_WIDTH ** -0.5),
        "a_sink": nrm(ks[7], (N_EVEN, A_Q_HEADS), 0.5),
        "c_w_in": nrm(ks[8], (N_ODD, D_MODEL, C_IN_WIDTH), D_MODEL ** -0.5),
        "c_w_out": nrm(ks[9], (N_ODD, D_MODEL, D_MODEL), D_MODEL ** -0.5),
        "c_lambda": nrm(ks[10], (N_ODD, 4, HEAD_DIM), 0.1),
        "c_subln_g": 1.0 + nrm(ks[11], (N_ODD, 2 * HEAD_DIM), 0.02),
        "ffn_w_up": nrm(ks[12], (DEPTH, D_MODEL, 2 * D_FF), D_MODEL ** -0.5),
        "ffn_conv_w": nrm(ks[13], (DEPTH, CONV_WIDTH, D_FF), CONV_WIDTH ** -0.5),
        "ffn_conv_b": nrm(ks[14], (DEPTH, D_FF), 0.02),
        "ffn_w_down": nrm(ks[15], (DEPTH, D_FF, D_MODEL), D_FF ** -0.5),
    }


def reference(x, c, ada_w, ada_b, norm_g, ab_w_in, ab_w_out, a_sink, c_w_in, c_w_out,
              c_lambda, c_subln_g, ffn_w_up, ffn_conv_w, ffn_conv_b, ffn_w_down):
    cond = jax.nn.silu(c)
    for layer in range(DEPTH):
        mod = cond @ ada_w[layer] + ada_b[layer]
        sh1, sc1, g1, sh2, sc2, g2 = [m[:, None, :] for m in jnp.split(mod, 6, axis=-1)]
        h = rms_norm(x, norm_g[layer, 0]) * (1.0 + sc1) + sh1
        j = layer // 2
        if layer % 2 == 0:
            y = local_dilated_mixer(h, ab_w_in[j], ab_w_out[j], a_sink[j])
        else:
            lambda_init = 0.8 - 0.6 * math.exp(-0.3 * layer)
            y = diff_attention(h, c_w_in[j], c_w_out[j], c_lambda[j], c_subln_g[j], lambda_init)
        x = x + g1 * rms_norm(y, norm_g[layer, 1])
        h = rms_norm(x, norm_g[layer, 2]) * (1.0 + sc2) + sh2
        y = conv_ffn(h, ffn_w_up[layer], ffn_conv_w[layer], ffn_conv_b[layer], ffn_w_down[layer])
        x = x + g2 * rms_norm(y, norm_g[layer, 3])
    return x
```

```python
from contextlib import ExitStack
import numpy as np
import concourse.bass as bass
import concourse.mybir as mybir

F32 = mybir.dt.float32
BF16 = mybir.dt.bfloat16
AF = mybir.ActivationFunctionType
ALU = mybir.AluOpType
AX = mybir.AxisListType

ENGS = ("pe", "act", "dve", "pool", "sp")


class Buf:
    __slots__ = ("ap", "w", "r", "semkey", "name", "t", "persistent")

    def __init__(self, name, t, ap):
        self.name = name
        self.t = t
        self.ap = ap
        self.w = {}
        self.r = {}
        self.semkey = None
        self.persistent = False

    def __getitem__(self, idx):
        return self.ap[idx]


class Prog:
    def __init__(self):
        self.nc = bass.Bass("TRN2", target_bir_lowering=False)
        self.stack = ExitStack()
        self.sems = {}
        self.semcnt = {}
        self.ops = {e: [] for e in ENGS}
        self.seen = {e: {} for e in ENGS}
        self.pending = {e: False for e in ENGS}
        self.outs = []
        self.nbuf = 0
        self.free_keys = []
        self.stage_bufs = []
        for e in ENGS[:4]:
            self._newsem(e)

    def _newsem(self, key):
        self.sems[key] = self.stack.enter_context(self.nc.semaphore("s_" + key))
        self.semcnt[key] = 0

    def sb(self, name, shape, dtype):
        t = self.stack.enter_context(self.nc.sbuf_tensor(name, list(shape), dtype))
        return Buf(name, t, t[:])

    def ps(self, name, shape, dtype):
        t = self.stack.enter_context(self.nc.psum_tensor(name, list(shape), dtype))
        return Buf(name, t, t[:])

    def dram(self, name, shape, dtype, kind="Internal"):
        t = self.nc.dram_tensor(name, list(shape), dtype, kind=kind)
        b = Buf(name, t, t.ap())
        if kind == "ExternalOutput":
            self.outs.append(b)
        return b

    def view(self, buf, ap):
        return ap

    def _need(self, eng, waits, key, val):
        if eng == "pe" and key == "pe":
            return
        if self.seen[eng].get(key, 0) >= val:
            return
        if waits.get(key, 0) < val:
            waits[key] = val

    def _deps(self, eng, reads, writes):
        waits = {}
        for b in reads:
            for k, v in b.w.items():
                self._need(eng, waits, k, v)
        for b in writes:
            for k, v in b.w.items():
                self._need(eng, waits, k, v)
            for k, v in b.r.items():
                self._need(eng, waits, k, v)
        for k, v in waits.items():
            self.seen[eng][k] = v
            self.ops[eng].append(("wait", k, v))

    def _mark(self, tok, reads, writes):
        k, v = tok
        for b in reads:
            if b.r.get(k, 0) < v:
                b.r[k] = v
        for b in writes:
            b.w[k] = v
            b.r.clear()

    def op(self, eng, fn, reads=(), writes=(), signal=True):
        self._deps(eng, reads, writes)
        if signal:
            self.semcnt[eng] += 1
            tok = (eng, self.semcnt[eng])
            self.pending[eng] = False
            self.ops[eng].append(("op", fn, eng, 1))
        else:
            tok = (eng, self.semcnt[eng] + 1)
            self.pending[eng] = True
            self.ops[eng].append(("op", fn, None, 0))
        self._mark(tok, reads, writes)

    def dma(self, q, out, in_, reads, writes, sembuf):
        key = self.semkey_for(sembuf)
        self._deps(q, reads, writes)
        self.semcnt[key] += 16
        tok = (key, self.semcnt[key])
        self.ops[q].append(("op", lambda e: e.dma_start(out=out, in_=in_), key, 16))
        self._mark(tok, reads, writes)

    def op_sem(self, eng, fn, reads, writes, key, inc):
        if key not in self.sems:
            self._newsem(key)
        self._deps(eng, reads, writes)
        self.semcnt[key] += inc
        self.ops[eng].append(("op", fn, key, inc))
        self._mark((key, self.semcnt[key]), reads, writes)

    def semkey_for(self, sembuf):
        if sembuf.semkey is None:
            if self.free_keys:
                sembuf.semkey = self.free_keys.pop()
            else:
                self.nbuf += 1
                sembuf.semkey = "d%d" % self.nbuf
                self._newsem(sembuf.semkey)
            self.stage_bufs.append(sembuf)
        return sembuf.semkey

    def release_stage_sems(self):
        for b in self.stage_bufs:
            if not getattr(b, "persistent", False):
                self.free_keys.append(b.semkey)
                b.semkey = None
        self.stage_bufs = [b for b in self.stage_bufs if b.semkey is not None]

    def barrier(self):
        for e in ENGS:
            assert not self.pending[e]
            waits = {}
            for k, v in self.semcnt.items():
                if v > 0 and k != "cc":
                    self._need(e, waits, k, v)
            for k, v in waits.items():
                self.seen[e][k] = v
                self.ops[e].append(("wait", k, v))

    def mm(self, out, lhsT, rhs, start, stop, R, W, signal=True):
        self.op("pe", lambda e: e.matmul(out, lhsT=lhsT, rhs=rhs, start=start, stop=stop), R, W, signal)

    def tr(self, out, in_, ident, R, W, signal=True):
        self.op("pe", lambda e: e.transpose(out, in_, ident), R, W, signal)

    def act(self, out, in_, func, R, W, bias=None, scale=None, accum_out=None, eng="act"):
        kw = {}
        if bias is not None:
            kw["bias"] = bias
        if scale is not None:
            kw["scale"] = scale
        if accum_out is not None:
            kw["accum_out"] = accum_out
        self.op(eng, lambda e: e.activation(out=out, in_=in_, func=func, **kw), R, W)

    def ts(self, eng, out, in0, s1, s2, op0, op1, R, W, accum_out=None):
        if op1 is None:
            self.op(eng, lambda e: e.tensor_scalar(out=out, in0=in0, scalar1=s1, scalar2=None, op0=op0), R, W)
        else:
            self.op(eng, lambda e: e.tensor_scalar(out=out, in0=in0, scalar1=s1, scalar2=s2, op0=op0, op1=op1), R, W)

    def stt(self, eng, out, in0, scalar, in1, op0, op1, R, W):
        self.op(eng, lambda e: e.scalar_tensor_tensor(out=out, in0=in0, scalar=scalar, in1=in1, op0=op0, op1=op1), R, W)

    def tt(self, eng, out, in0, in1, op, R, W):
        self.op(eng, lambda e: e.tensor_tensor(out=out, in0=in0, in1=in1, op=op), R, W)

    def copy(self, eng, out, in_, R, W):
        if eng == "act":
            self.op(eng, lambda e: e.copy(out=out, in_=in_), R, W)
        else:
            self.op(eng, lambda e: e.tensor_copy(out=out, in_=in_), R, W)

    def memset(self, eng, ap, val, W):
        self.op(eng, lambda e: e.memset(ap, val), (), W)

    def finish(self):
        waits = {}
        for b in self.outs:
            for k, v in b.w.items():
                self._need("sp", waits, k, v)
        for k, v in waits.items():
            self.seen["sp"][k] = v
            self.ops["sp"].append(("wait", k, v))
        for e in ENGS:
            assert not self.pending[e], e
        nc = self.nc
        sems = self.sems

        def replay(lst):
            def body(e):
                for it in lst:
                    if it[0] == "wait":
                        e.wait_ge(sems[it[1]], it[2])
                    else:
                        ins = it[1](e)
                        if it[3]:
                            ins.then_inc(sems[it[2]], it[3])
            return body

        with nc.Block() as block:
            block.tensor(replay(self.ops["pe"]))
            block.scalar(replay(self.ops["act"]))
            block.vector(replay(self.ops["dve"]))
            block.gpsimd(replay(self.ops["pool"]))
            block.sync(replay(self.ops["sp"]))
        self.stack.close()
        return nc
import ml_dtypes

D = 2048
DC = 16
T = 1024
NT = 8
EPS = 1e-6
HD = 128
SCALE = HD ** -0.5


def emit_mod_prep(P, modT, ngT, gi, si, ni, GT, SHT):
    P.stt("dve", GT.ap, modT[:, si, :], 1.0, ngT[:, ni, :], ALU.add, ALU.mult, [modT, ngT], [GT])
    P.copy("dve", SHT.ap, modT[:, gi, :], [modT], [SHT])


def emit_norm_T(P, x_dram, row0, nrows, hT, col0, GT, SHT, ident, xb, xnb, junk, stat, pst, ti):
    xt = xb[ti % 2]
    xn = xnb[ti % 2]
    if x_dram is not None:
        P.dma("sp", xt[0:nrows, :], x_dram[row0:row0 + nrows, :], [x_dram], [xt], xt)
    st = stat[ti % len(stat)]
    ss = st[:, 0:1]
    rs = st[:, 1:2]
    P.act(junk.ap, xt.ap, AF.Square, [xt], [junk, st], accum_out=ss)
    P.ts("dve", rs, ss, 1.0 / D, EPS, ALU.mult, ALU.add, [st], [st])
    P.act(rs, rs, AF.Sqrt, [st], [st])
    P.op("dve", lambda e: e.reciprocal(out=rs, in_=rs), [st], [st])
    P.ts("dve", xn.ap, xt.ap, rs, None, ALU.mult, None, [xt, st], [xn])
    for g in range(2):
        pt = pst[g]
        for i in range(8):
            c = g * 8 + i
            P.tr(pt[:, i, :], xn[:, c * 128:(c + 1) * 128], ident.ap, [xn, ident], [pt], signal=(i == 7))
        for i in range(8):
            c = g * 8 + i
            dst = hT[:, c, col0:col0 + nrows]
            if i % 2 == 0:
                P.act(dst, pt[:, i, 0:nrows], AF.Identity, [pt, GT, SHT], [hT],
                      bias=SHT[:, c:c + 1], scale=GT[:, c:c + 1])
            else:
                P.ts("dve", dst, pt[:, i, 0:nrows], GT[:, c:c + 1], SHT[:, c:c + 1], ALU.mult, ALU.add,
                     [pt, GT, SHT], [hT])


def build_LA(kind):
    P = Prog()
    W = 3584 if kind == "ab" else 6144
    NQ = 16
    NK = 6 if kind == "ab" else 16
    NV = 768 if kind == "ab" else 2048
    x = P.dram("x", [T, D], F32, "ExternalInput")
    modT = P.dram("modT", [128, 6, 16], F32, "ExternalInput")
    ngT = P.dram("ngT", [128, 4, 16], F32, "ExternalInput")
    w_in = P.dram("w_in", [D, W], F32, "ExternalInput")
    identd = P.dram("ident", [128, 128], BF16, "ExternalInput")
    qT = P.dram("qT", [NQ, 128, T], BF16, "ExternalOutput")
    kT = P.dram("kT", [NK, 128, T], BF16, "ExternalOutput")
    v = P.dram("v", [T, NV], BF16, "ExternalOutput")

    ident = P.sb("ident_s", [128, 128], BF16)
    modS = P.sb("modS", [128, 6, 16], F32)
    ngS = P.sb("ngS", [128, 4, 16], F32)
    GT = P.sb("GT", [128, 16], F32)
    SHT = P.sb("SHT", [128, 16], F32)
    hT = P.sb("hT", [128, DC, T], BF16)
    xb = [P.sb("xb%d" % i, [128, D], F32) for i in range(2)]
    xnb = [P.sb("xnb%d" % i, [128, D], BF16) for i in range(2)]
    junk = P.sb("junk", [128, D], BF16)
    stat = P.sb("stat", [128, 32], F32)
    pst = [P.ps("pst%d" % i, [128, 8, 128], BF16) for i in range(2)]
    psm = [P.ps("psm%d" % i, [128, 512], F32) for i in range(4)]
    wb = [P.sb("wb%d" % i, [128, DC, 512], BF16) for i in range(3)]
    stg_f = [P.sb("stgf%d" % i, [128, T], BF16) for i in range(2)]
    stg_t = [P.sb("stgt%d" % i, [128, 512], BF16) for i in range(3)]

    P.dma("sp", ident.ap, identd.ap, [identd], [ident], ident)
    P.dma("sp", modS.ap, modT.ap, [modT], [modS], modS)
    P.dma("sp", ngS.ap, ngT.ap, [ngT], [ngS], ngS)
    P.memset("dve", stat.ap, 0.0, [stat])
    emit_mod_prep(P, modS, ngS, 0, 1, 0, GT, SHT)

    if kind == "ab":
        blocks = [
            (0, [(0, ("q", 0)), (128, ("q", 1)), (256, ("q", 2)), (384, ("q", 3))], []),
            (512, [(0, ("k", 0)), (128, ("k", 1))], [(256, 256, 0)]),
            (1024, [(i * 128, ("q", 4 + i)) for i in range(4)], []),
            (1536, [(i * 128, ("q", 8 + i)) for i in range(4)], []),
            (2048, [(i * 128, ("q", 12 + i)) for i in range(4)], []),
            (2560, [(i * 128, ("k", 2 + i)) for i in range(4)], []),
            (3072, [], [(0, 512, 256)]),
        ]
    else:
        blocks = []
        for i in range(4):
            blocks.append((i * 512, [(j * 128, ("q", i * 4 + j)) for j in range(4)], []))
        for i in range(4):
            blocks.append((2048 + i * 512, [(j * 128, ("k", i * 4 + j)) for j in range(4)], []))
        for i in range(4):
            blocks.append((4096 + i * 512, [], [(0, 512, i * 512)]))
    w3 = w_in.ap.rearrange("(c p) n -> p c n", p=128)

    def load_w(bi):
        c0 = blocks[bi][0]
        b = wb[bi % 3]
        P.dma("pool", b.ap, w3[:, :, c0:c0 + 512], [w_in], [b], b)

    load_w(0)
    load_w(1)
    for t in range(NT):
        emit_norm_T(P, x, t * 128, 128, hT, t * 128, GT, SHT, ident, xb, xnb, junk, stat, pst, t)

    pi = 0
    fi = 0
    ti = 0
    for bi, (c0, funits, tranges) in enumerate(blocks):
        if bi + 2 < len(blocks):
            load_w(bi + 2)
        b = wb[bi % 3]
        for (off, (which, idx)) in funits:
            sg = stg_f[fi % 2]
            fi += 1
            for half in range(2):
                pp = psm[pi % 4]
                pi += 1
                for c in range(DC):
                    P.mm(pp.ap, b[:, c, off:off + 128], hT[:, c, half * 512:(half + 1) * 512],
                         c == 0, c == DC - 1, [b, hT], [pp], signal=(c == DC - 1))
                dst = sg[:, half * 512:(half + 1) * 512]
                sc = SCALE if which == "q" else 1.0
                if half == 0:
                    P.act(dst, pp.ap, AF.Copy, [pp], [sg], scale=sc)
                else:
                    P.ts("dve", dst, pp.ap, sc, None, ALU.mult, None, [pp], [sg])
            dd = qT if which == "q" else kT
            P.dma("sp", dd.ap[idx], sg.ap, [sg], [dd], sg)
        for (off, wdt, dcol) in tranges:
            for t in range(NT):
                pp = psm[pi % 4]
                pi += 1
                for c in range(DC):
                    P.mm(pp[:, 0:wdt], hT[:, c, t * 128:(t + 1) * 128], b[:, c, off:off + wdt],
                         c == 0, c == DC - 1, [b, hT], [pp], signal=(c == DC - 1))
                sg = stg_t[ti % 3]
                ti += 1
                if t % 2 == 0:
                    P.act(sg[:, 0:wdt], pp[:, 0:wdt], AF.Copy, [pp], [sg])
                else:
                    P.copy("dve", sg[:, 0:wdt], pp[:, 0:wdt], [pp], [sg])
                P.dma("sp", v.ap[t * 128:(t + 1) * 128, dcol:dcol + wdt], sg[:, 0:wdt], [sg], [v], sg)
    return P.finish()


class Arena:
    def __init__(self, P, name, words):
        self.P = P
        self.buf = P.sb(name, [128, words], F32)
        self.words = words
        self.off = 0
        self.n = 0

    def reset(self, off=0):
        self.off = off

    def take(self, name, shape, dtype):
        n = 1
        for s in shape[1:]:
            n *= s
        words = n if dtype == F32 else (n + 1) // 2
        a = self.off
        self.off += words
        assert self.off <= self.words, (name, self.off, self.words)
        ap = self.buf.ap[:, a:a + words]
        if dtype != F32:
            ap = ap.bitcast(dtype)
            if n % 2:
                ap = ap[:, 0:n]
        if len(shape) == 3:
            ap = ap.rearrange("p (a b) -> p a b", b=shape[2])
        self.n += 1
        return Buf("%s_%d" % (name, self.n), self.buf.t, ap)


DFF = 5504
NFC = 43


def build_LC():
    P = Prog()
    xm = P.dram("xm", [T, D], F32, "ExternalInput")
    halo = P.dram("halo", [2, D], F32, "ExternalInput")
    flags = P.dram("flags", [128, 2], F32, "ExternalInput")
    modT = P.dram("modT", [128, 6, 16], F32, "ExternalInput")
    ngT = P.dram("ngT", [128, 4, 16], F32, "ExternalInput")
    rows = P.dram("rows", [2, D], F32, "ExternalInput")
    w_up = P.dram("w_up", [D, 2 * DFF], F32, "ExternalInput")
    convT = P.dram("convT", [128, 4, NFC], F32, "ExternalInput")
    w_down = P.dram("w_down", [DFF, D], F32, "ExternalInput")
    identd = P.dram("ident", [128, 128], BF16, "ExternalInput")
    xo = P.dram("xo", [T, D], F32, "ExternalOutput")
    yscr = P.dram("yscr", [T, D], F32)

    ident = P.sb("ident_s", [128, 128], BF16)
    modS = P.sb("modS", [128, 6, 16], F32)
    ngS = P.sb("ngS", [128, 4, 16], F32)
    GT = P.sb("GT", [128, 16], F32)
    SHT = P.sb("SHT", [128, 16], F32)
    convS = P.sb("convS", [128, 4, NFC], F32)
    flagS = P.sb("flagS", [128, 2], F32)
    stat = P.sb("stat", [128, 32], F32)
    aT = P.sb("aT", [128, NFC, T], BF16)
    banks = [P.ps("bank%d" % i, [128, 512], F32) for i in range(8)]
    A = Arena(P, "arena", 24064)

    P.dma("sp", ident.ap, identd.ap, [identd], [ident], ident)
    P.dma("sp", modS.ap, modT.ap, [modT], [modS], modS)
    P.dma("sp", ngS.ap, ngT.ap, [ngT], [ngS], ngS)
    P.dma("sp", convS.ap, convT.ap, [convT], [convS], convS)
    P.dma("sp", flagS.ap, flags.ap, [flags], [flagS], flagS)
    P.memset("dve", stat.ap, 0.0, [stat])
    emit_mod_prep(P, modS, ngS, 3, 4, 2, GT, SHT)

    h2T = A.take("h2T", [128, DC, T + 2], BF16)
    wg = [A.take("wg", [128, DC, 256], BF16) for _ in range(2)]
    wu = [A.take("wu", [128, DC, 256], BF16) for _ in range(2)]
    mark = A.off
    xb = [A.take("xb", [128, D], F32) for _ in range(2)]
    xnb = [A.take("xnb", [128, D], BF16) for _ in range(2)]
    junk = A.take("junk", [128, D], BF16)
    w3 = w_up.ap.rearrange("(c p) n -> p c n", p=128)
    NG = (NFC + 1) // 2

    def load_wu(g):
        c0 = g * 256
        wd_ = min(256, DFF - c0)
        P.dma("pool", wg[g % 2][:, :, 0:wd_], w3[:, :, c0:c0 + wd_], [w_up], [wg[g % 2]], wg[g % 2])
        P.dma("pool", wu[g % 2][:, :, 0:wd_], w3[:, :, DFF + c0:DFF + c0 + wd_], [w_up], [wu[g % 2]], wu[g % 2])

    load_wu(0)
    load_wu(1)
    pst = []
    for i in (6, 7):
        b = banks[i]
        pst.append(Buf("pst%d" % i, b.t, b.ap.bitcast(BF16).rearrange("p (a b) -> p a b", b=128)))
        pst[-1].w = b.w
        pst[-1].r = b.r
    for t in range(NT):
        emit_norm_T(P, xm, t * 128, 128, h2T, t * 128, GT, SHT, ident, xb, xnb, junk, stat, pst, t)
    emit_norm_T(P, halo, 0, 2, h2T, T, GT, SHT, ident, xb, xnb, junk, stat, pst, NT)
    P.barrier()
    for b in banks:
        b.w.clear()
        b.r.clear()

    A.reset(mark)
    gsb = [A.take("gsb", [128, T + 4], F32) for _ in range(2)]
    acc = [A.take("acc", [128, T], F32) for _ in range(2)]
    gg = acc
    usb = [A.take("usb", [128, T], F32) for _ in range(2)]
    sets = [(banks[0], banks[1]), (banks[2], banks[3]), (banks[4], banks[5])]
    si = 0
    ph = banks[6]
    for fc in range(NFC):
        g = fc // 2
        off = (fc % 2) * 128
        if fc % 2 == 0 and g >= 1 and g + 1 < NG:
            load_wu(g + 1)
        wgb, wub = wg[g % 2], wu[g % 2]
        gs, ac, ggb, us = gsb[fc % 2], acc[fc % 2], gg[fc % 2], usb[fc % 2]
        hc = (fc % 8) * 2
        sg = sets[si % 3]
        si += 1
        for half in range(2):
            for c in range(DC):
                P.mm(sg[half].ap, wgb[:, c, off:off + 128], h2T[:, c, half * 512:(half + 1) * 512],
                     c == 0, c == DC - 1, [wgb, h2T], [sg[half]], signal=(c == DC - 1))
        for c in range(DC):
            P.mm(ph[:, hc:hc + 2], wgb[:, c, off:off + 128], h2T[:, c, T:T + 2],
                 c == 0, c == DC - 1, [wgb, h2T], [ph], signal=(c == DC - 1))
        P.act(gs[:, 1:513], sg[0].ap, AF.Copy, [sg[0]], [gs])
        P.act(gs[:, 513:1025], sg[1].ap, AF.Copy, [sg[1]], [gs])
        P.tt("dve", gs[:, 0:1], ph[:, hc:hc + 1], flagS[:, 0:1], ALU.mult, [ph, flagS], [gs])
        P.tt("dve", gs[:, 1025:1026], ph[:, hc + 1:hc + 2], flagS[:, 1:2], ALU.mult, [ph, flagS], [gs])
        su = sets[si % 3]
        si += 1
        for half in range(2):
            for c in range(DC):
                P.mm(su[half].ap, wub[:, c, off:off + 128], h2T[:, c, half * 512:(half + 1) * 512],
                     c == 0, c == DC - 1, [wub, h2T], [su[half]], signal=(c == DC - 1))
        P.act(us[:, 0:512], su[0].ap, AF.Copy, [su[0]], [us])
        P.act(us[:, 512:1024], su[1].ap, AF.Copy, [su[1]], [us])
        P.ts("dve", ac.ap, gs[:, 0:T], convS[:, 0, fc:fc + 1], None, ALU.mult, None, [gs, convS], [ac])
        P.stt("dve", ac.ap, gs[:, 1:T + 1], convS[:, 1, fc:fc + 1], ac.ap, ALU.mult, ALU.add, [gs, convS, ac], [ac])
        P.stt("dve", ac.ap, gs[:, 2:T + 2], convS[:, 2, fc:fc + 1], ac.ap, ALU.mult, ALU.add, [gs, convS, ac], [ac])
        P.act(ggb.ap, ac.ap, AF.Gelu_apprx_tanh, [ac, convS], [ggb], bias=convS[:, 3, fc:fc + 1])
        P.tt("pool", aT[:, fc, :], ggb.ap, us.ap, ALU.mult, [ggb, us], [aT])
    P.barrier()

    A.reset(0)
    wd = [A.take("wd", [128, NFC, 256], BF16) for _ in range(2)]
    yst = [A.take("yst", [128, 256], F32) for _ in range(4)]
    ggb_ = A.take("GGb", [128, D], F32)
    yt = [A.take("yt", [128, D], F32) for _ in range(2)]
    xt2 = [A.take("xt2", [128, D], F32) for _ in range(2)]
    junk2 = A.take("junk2", [128, D], BF16)
    grow = yt[0]
    wd3 = w_down.ap.rearrange("(c p) n -> p c n", p=128)

    def load_wd(nb):
        P.dma("pool", wd[nb % 2].ap, wd3[:, :, nb * 256:(nb + 1) * 256], [w_down], [wd[nb % 2]], wd[nb % 2])

    load_wd(0)
    load_wd(1)
    P.dma("sp", ggb_.ap, rows.ap[0:1, :].partition_broadcast(128), [rows], [ggb_], ggb_)
    P.dma("sp", grow.ap, rows.ap[1:2, :].partition_broadcast(128), [rows], [grow], grow)
    P.tt("pool", ggb_.ap, ggb_.ap, grow.ap, ALU.mult, [ggb_, grow], [ggb_])
    bi = 0
    for nb in range(8):
        wdb = wd[nb % 2]
        for t in range(NT):
            pp = banks[bi % 6]
            ys = yst[bi % 4]
            bi += 1
            for fc in range(NFC):
                P.mm(pp[:, 0:256], aT[:, fc, t * 128:(t + 1) * 128], wdb[:, fc, :],
                     fc == 0, fc == NFC - 1, [aT, wdb], [pp], signal=(fc == NFC - 1))
            if bi % 2 == 0:
                P.act(ys.ap, pp[:, 0:256], AF.Copy, [pp], [ys])
            else:
                P.copy("dve", ys.ap, pp[:, 0:256], [pp], [ys])
            P.dma("sp", yscr.ap[t * 128:(t + 1) * 128, nb * 256:(nb + 1) * 256], ys.ap, [ys], [yscr], ys)
        if nb + 2 < 8:
            load_wd(nb + 2)

    emit_resid(P, yscr, xm, xo, ggb_, yt, xt2, stat, junk2)
    return P.finish()


def emit_resid(P, ysrc, xsrc, xdst, GGb, yt, xt2, stat, junk):
    for t in range(NT):
        y = yt[t % 2]
        xx = xt2[t % 2]
        P.dma("sp", y.ap, ysrc.ap[t * 128:(t + 1) * 128, :], [ysrc], [y], y)
        P.dma("sp", xx.ap, xsrc.ap[t * 128:(t + 1) * 128, :], [xsrc], [xx], xx)
        st = stat[t % len(stat)]
        ss = st[:, 2:3]
        rs = st[:, 3:4]
        P.act(junk.ap, y.ap, AF.Square, [y], [junk, st], accum_out=ss)
        P.ts("dve", rs, ss, 1.0 / D, EPS, ALU.mult, ALU.add, [st], [st])
        P.act(rs, rs, AF.Sqrt, [st], [st])
        P.op("dve", lambda e, rs=rs: e.reciprocal(out=rs, in_=rs), [st], [st])
        P.stt("dve", y.ap, y.ap, rs, GGb.ap, ALU.mult, ALU.mult, [y, st, GGb], [y])
        P.tt("dve", xx.ap, xx.ap, y.ap, ALU.add, [xx, y], [xx])
        P.dma("sp", xdst.ap[t * 128:(t + 1) * 128, :], xx.ap, [xx], [xdst], xx)


SLOPES16 = [2.0 ** (-8.0 * (i + 1) / 16) for i in range(16)]
SLOPES8 = [2.0 ** (-8.0 * (i + 1) / 8) for i in range(8)]
TYPES = {"A": (128, 1, 128, -512), "g0": (64, 1, 128, -512), "g1": (256, 4, 256, -640), "g2": (1024, 16, 1024, -1408)}
BIG = 1e30


def ltile_host(tp):
    R_, dil, dmax, dmin = TYPES[tp]
    wid = dmax - dmin + 512
    i = np.arange(128)[:, None]
    c = np.arange(wid)[None, :]
    d = c + dmin - i
    ok = (np.abs(d) <= R_) & (d % dil == 0)
    return np.where(ok, np.abs(d), BIG).astype(np.float32)


def lc_host(qr):
    i = np.arange(128)[:, None]
    c = np.arange(4992)[None, :]
    return np.abs(c - 3968 + qr * 1024 - i).astype(np.float32)


def build_LB(kind):
    P = Prog()
    ab = kind == "ab"
    KC = 8 if ab else 16
    VW = 129 if ab else 257
    qT = P.dram("qT", [16, 128, T], BF16, "ExternalInput")
    if ab:
        kT = P.dram("kT", [6, 128, 3072], BF16, "ExternalInput")
        vv = P.dram("vv", [3072, 6, 129], BF16, "ExternalInput")
        Ld = {tp: P.dram("L" + tp, [128, TYPES[tp][2] - TYPES[tp][3] + 512], F32, "ExternalInput") for tp in TYPES}
        sinkd = P.dram("sinkb", [128, 4], F32, "ExternalInput")
    else:
        kT = P.dram("kT", [16, 128, 4096], BF16, "ExternalInput")
        vv = P.dram("vv", [4096, 2048], BF16, "ExternalInput")
        Lcd = P.dram("Lc", [128, 4992], F32, "ExternalInput")
        lambd = P.dram("lamb", [128, 4, 128], F32, "ExternalInput")
        sgd = P.dram("sublnb", [128, 256], F32, "ExternalInput")
        lind = P.dram("lin", [128, 2], F32, "ExternalInput")
    x = P.dram("x", [T, D], F32, "ExternalInput")
    rows = P.dram("rows", [2, D], F32, "ExternalInput")
    w_out = P.dram("w_out", [KC * 128, D], F32, "ExternalInput")
    identd = P.dram("ident", [128, 128], BF16, "ExternalInput")
    xo = P.dram("xo", [T, D], F32, "ExternalOutput")
    yscr = P.dram("yscr", [T, D], F32)

    ident = P.sb("ident_s", [128, 128], BF16)
    stat = P.sb("stat", [128, 32], F32)
    oT = P.sb("oT_all", [128, KC, T], BF16)
    tmp = [P.sb("tmp%d" % i, [128, 512], F32) for i in range(2)]
    PT = [P.sb("PT%d" % i, [128, 512], BF16) for i in range(2)]
    sm = P.sb("small", [128, 64], F32)
    on = [P.sb("on%d" % i, [128, 4, VW - 1], BF16) for i in range(2)]
    banks = [P.ps("bank%d" % i, [128, 512], F32) for i in range(8)]
    sbank = banks[0:2]
    obank = banks[2:6]
    tb = banks[6]
    tbv = Buf("tbv", tb.t, tb.ap.bitcast(BF16).rearrange("p (a b) -> p a b", b=128))
    tbv.w = tb.w
    tbv.r = tb.r
    P.dma("sp", ident.ap, identd.ap, [identd], [ident], ident)
    P.memset("dve", stat.ap, 0.0, [stat])
    P.memset("dve", sm.ap, 0.0, [sm])

    def blocks(jobs, vw, first=True, last=True):
        n = len(jobs)
        cnt = blocks.cnt
        for i in range(n + 1):
            if i < n:
                kap, qap, vap, lap, slope, Rl = jobs[i]
                S = sbank[(cnt + i) % 2]
                tm = tmp[(cnt + i) % 2]
                pt = PT[(cnt + i) % 2]
                P.mm(S.ap, kap, qap, True, True, Rl, [S])
                P.stt("dve", tm.ap, lap, -slope, S.ap, ALU.mult, ALU.add, [S] + Rl, [tm])
                P.act(pt.ap, tm.ap, AF.Exp, [tm], [pt])
            if i >= 1:
                kap, qap, vap, lap, slope, Rl = jobs[i - 1]
                pt = PT[(cnt + i - 1) % 2]
                for qs in range(4):
                    P.mm(obank[qs][:, 0:vw], pt[:, qs * 128:(qs + 1) * 128], vap,
                         first and i == 1, last and i == n, [pt] + Rl, [obank[qs]],
                         signal=(qs == 3))
        blocks.cnt = cnt + n
    blocks.cnt = 0

    if ab:
        A = Arena(P, "arena", 26800)
        qa = A.take("qall", [128, 16, T], BF16)
        ka = A.take("kall", [128, 6, 3072], BF16)
        va = A.take("vall", [128, 24, 6 * 129], BF16)
        Ls = {tp: P.sb("Ls" + tp, [128, TYPES[tp][2] - TYPES[tp][3] + 512], F32) for tp in TYPES}
        P.dma("sp", qa.ap, qT.ap.rearrange("h p t -> p h t"), [qT], [qa], qa)
        P.dma("sp", ka.ap, kT.ap.rearrange("h p t -> p h t"), [kT], [ka], ka)
        P.dma("sp", va.ap, vv.ap.rearrange("(kt p) h n -> p kt (h n)", p=128), [vv], [va], va)
        for tp in TYPES:
            P.dma("sp", Ls[tp].ap, Ld[tp].ap, [Ld[tp]], [Ls[tp]], Ls[tp])
        P.dma("sp", sm[:, 0:4], sinkd.ap, [sinkd], [sm], sm)
        P.act(sm[:, 4:8], sm[:, 0:4], AF.Exp, [sm], [sm])
        outs = []
        for u in range(4):
            outs.append((u, [(u, "A", SLOPES16[u])], u // 2, u))
        for i in range(4):
            outs.append((4 + i, [(4 + i, "g0", SLOPES16[4 + i]), (8 + i, "g1", SLOPES16[8 + i]),
                                 (12 + i, "g2", SLOPES16[12 + i])], 2 + i, None))
        fin = 0
        for (ou, qlist, kv, sink) in outs:
            for qb in range(2):
                q0 = qb * 512
                jobs = []
                for (qu, tp, slope) in qlist:
                    R_, dil, dmax, dmin = TYPES[tp]
                    for dl in range(dmax, dmin - 1, -128):
                        wk = (q0 - dl + 1024) // 128
                        jobs.append((ka[:, kv, wk * 128:(wk + 1) * 128], qa[:, qu, q0:q0 + 512],
                                     va[:, wk, kv * 129:(kv + 1) * 129], Ls[tp][:, dl - dmin:dl - dmin + 512],
                                     slope, [ka, qa, va, Ls[tp]]))
                blocks(jobs, 129)
                onb = on[fin % 2]
                fin += 1
                for qs in range(4):
                    dn = sm[:, 8 + qs:9 + qs]
                    if sink is not None:
                        P.tt("dve", dn, obank[qs][:, 128:129], sm[:, 4 + sink:5 + sink], ALU.add, [obank[qs], sm], [sm])
                    else:
                        P.copy("dve", dn, obank[qs][:, 128:129], [obank[qs]], [sm])
                    P.op("dve", lambda e, dn=dn: e.reciprocal(out=dn, in_=dn), [sm], [sm])
                    P.ts("dve", onb[:, qs, :], obank[qs][:, 0:128], dn, None, ALU.mult, None, [obank[qs], sm], [onb])
                for qs in range(4):
                    P.tr(tbv[:, qs, :], onb[:, qs, :], ident.ap, [onb, ident], [tbv], signal=(qs == 3))
                P.act(oT[:, ou, q0:q0 + 512], tbv.ap[:, 0:4, :].rearrange("p a b -> p (a b)"), AF.Copy, [tbv], [oT])
    else:
        A = Arena(P, "arena", 20480)
        kh = [A.take("kh", [128, 2, 4096], BF16) for _ in range(2)]
        vh = [A.take("vh", [128, 32, 257], BF16) for _ in range(2)]
        qh = [A.take("qh", [128, 2, T], BF16) for _ in range(2)]
        Lc = P.sb("Lc_s", [128, 4992], F32)
        o1 = P.sb("o1", [128, 4, 257], F32)
        of = P.sb("of", [128, 256], F32)
        jk = P.sb("jk", [128, 256], BF16)
        lam = P.sb("lam_s", [128, 4, 128], F32)
        SG = P.sb("SG", [128, 256], F32)
        P.dma("sp", Lc.ap, Lcd.ap, [Lcd], [Lc], Lc)
        P.dma("sp", lam.ap, lambd.ap, [lambd], [lam], lam)
        P.dma("sp", SG.ap, sgd.ap, [sgd], [SG], SG)
        P.dma("sp", sm[:, 0:2], lind.ap, [lind], [sm], sm)
        for j in range(2):
            P.tt("dve", lam[:, 2 * j, :], lam[:, 2 * j, :], lam[:, 2 * j + 1, :], ALU.mult, [lam], [lam])
            P.op("dve", lambda e, j=j: e.tensor_reduce(out=sm[:, 4 + j:5 + j], in_=lam[:, 2 * j, :], axis=AX.X, op=ALU.add),
                 [lam], [sm])
        P.act(sm[:, 4:6], sm[:, 4:6], AF.Exp, [sm], [sm])
        P.tt("dve", sm[:, 2:3], sm[:, 4:5], sm[:, 5:6], ALU.subtract, [sm], [sm])
        P.tt("dve", sm[:, 2:3], sm[:, 2:3], sm[:, 0:1], ALU.add, [sm], [sm])
        P.ts("dve", sm[:, 3:4], sm[:, 2:3], -1.0, None, ALU.mult, None, [sm], [sm])
        P.ts("dve", SG.ap, SG.ap, sm[:, 1:2], None, ALU.mult, None, [SG, sm], [SG])
        vr = vv.ap.rearrange("(kt p) n -> p kt n", p=128)
        for i in range(2):
            P.memset("pool", vh[i][:, :, 256:257], 1.0, [vh[i]])

        def load_head(h):
            P.dma("sp", kh[h % 2].ap, kT.ap[2 * h:2 * h + 2].rearrange("j p t -> p j t"), [kT], [kh[h % 2]], kh[h % 2])
            P.dma("sp", vh[h % 2][:, :, 0:256], vr[:, :, h * 256:(h + 1) * 256], [vv], [vh[h % 2]], vh[h % 2])
            P.dma("sp", qh[h % 2].ap, qT.ap[2 * h:2 * h + 2].rearrange("j p t -> p j t"), [qT], [qh[h % 2]], qh[h % 2])

        load_head(0)
        fin = 0
        for h in range(8):
            if h + 1 < 8:
                load_head(h + 1)
            khb, vhb, qhb = kh[h % 2], vh[h % 2], qh[h % 2]
            for qb in range(2):
                q0 = qb * 512
                for j in range(2):
                    jobs = []
                    for kt in range(32):
                        c0 = q0 - kt * 128 + 3968
                        jobs.append((khb[:, j, kt * 128:(kt + 1) * 128], qhb[:, j, q0:q0 + 512], vhb[:, kt, :],
                                     Lc[:, c0:c0 + 512], SLOPES8[h], [khb, qhb, vhb, Lc]))
                    blocks(jobs, 257)
                    if j == 0:
                        for qs in range(4):
                            P.act(o1[:, qs, :], obank[qs][:, 0:257], AF.Copy, [obank[qs]], [o1])
                onb = on[fin % 2]
                fin += 1
                for qs in range(4):
                    r1 = sm[:, 8 + 4 * qs:9 + 4 * qs]
                    r2 = sm[:, 9 + 4 * qs:10 + 4 * qs]
                    ss = sm[:, 10 + 4 * qs:11 + 4 * qs]
                    rs = sm[:, 11 + 4 * qs:12 + 4 * qs]
                    P.op("dve", lambda e, r1=r1, qs=qs: e.reciprocal(out=r1, in_=o1[:, qs, 256:257]), [o1], [sm])
                    P.op("dve", lambda e, r2=r2, qs=qs: e.reciprocal(out=r2, in_=obank[qs][:, 256:257]), [obank[qs]], [sm])
                    P.tt("dve", r2, r2, sm[:, 3:4], ALU.mult, [sm], [sm])
                    P.ts("dve", of.ap, o1[:, qs, 0:256], r1, None, ALU.mult, None, [o1, sm], [of])
                    P.stt("dve", of.ap, obank[qs][:, 0:256], r2, of.ap, ALU.mult, ALU.add, [obank[qs], sm, of], [of])
                    P.memset("dve", ss, 0.0, [sm])
                    P.act(jk.ap, of.ap, AF.Square, [of], [jk, sm], accum_out=ss)
                    P.ts("dve", rs, ss, 1.0 / 256, EPS, ALU.mult, ALU.add, [sm], [sm])
                    P.act(rs, rs, AF.Sqrt, [sm], [sm])
                    P.op("dve", lambda e, rs=rs: e.reciprocal(out=rs, in_=rs), [sm], [sm])
                    P.stt("dve", onb[:, qs, :], of.ap, rs, SG.ap, ALU.mult, ALU.mult, [of, sm, SG], [onb])
                for qs in range(4):
                    for i in range(2):
                        P.tr(tbv[:, qs * 2 + i, :], onb[:, qs, i * 128:(i + 1) * 128], ident.ap, [onb, ident], [tbv],
                             signal=(qs == 3 and i == 1))
                tv = tbv.ap.rearrange("p (a i) b -> p i a b", i=2)
                for i in range(2):
                    P.act(oT[:, 2 * h + i, q0:q0 + 512].rearrange("p (a b) -> p a b", b=128), tv[:, i, :, :], AF.Copy, [tbv], [oT])

    P.barrier()
    A.reset(0)
    wo = [A.take("wo", [128, KC, 512], BF16) for _ in range(2)]
    yst = [A.take("yst", [128, 512], F32) for _ in range(2)]
    ggb_ = A.take("GGb", [128, D], F32)
    yt = [A.take("yt", [128, D], F32) for _ in range(2)]
    xt2 = [A.take("xt2", [128, D], F32) for _ in range(2)]
    junk2 = A.take("junk2", [128, D], BF16)
    wo3 = w_out.ap.rearrange("(c p) n -> p c n", p=128)

    def load_wo(nb):
        P.dma("pool", wo[nb % 2].ap, wo3[:, :, nb * 512:(nb + 1) * 512], [w_out], [wo[nb % 2]], wo[nb % 2])

    load_wo(0)
    load_wo(1)
    P.dma("sp", ggb_.ap, rows.ap[0:1, :].partition_broadcast(128), [rows], [ggb_], ggb_)
    P.dma("sp", yt[0].ap, rows.ap[1:2, :].partition_broadcast(128), [rows], [yt[0]], yt[0])
    P.tt("pool", ggb_.ap, ggb_.ap, yt[0].ap, ALU.mult, [ggb_, yt[0]], [ggb_])
    for b in banks:
        b.w.clear()
        b.r.clear()
    bi = 0
    for nb in range(4):
        wob = wo[nb % 2]
        for t in range(NT):
            pp = banks[bi % 8]
            ys = yst[bi % 2]
            bi += 1
            for c in range(KC):
                P.mm(pp.ap, oT[:, c, t * 128:(t + 1) * 128], wob[:, c, :], c == 0, c == KC - 1, [oT, wob], [pp],
                     signal=(c == KC - 1))
            if bi % 2 == 0:
                P.act(ys.ap, pp.ap, AF.Copy, [pp], [ys])
            else:
                P.copy("dve", ys.ap, pp.ap, [pp], [ys])
            P.dma("sp", yscr.ap[t * 128:(t + 1) * 128, nb * 512:(nb + 1) * 512], ys.ap, [ys], [yscr], ys)
        if nb + 2 < 4:
            load_wo(nb + 2)
    emit_resid(P, yscr, x, xo, ggb_, yt, xt2, stat, junk2)
    return P.finish()


def build_L0():
    P = Prog()
    cT = P.dram("cT", [128, 16, 2], F32, "ExternalInput")
    w = P.dram("w", [D, 6144], F32, "ExternalInput")
    bias = P.dram("bias", [2, 6144], F32, "ExternalInput")
    mod = P.dram("mod", [2, 6144], F32, "ExternalOutput")
    cS = P.sb("cS", [128, 16, 2], F32)
    bS = P.sb("bS", [2, 6144], F32)
    oS = P.sb("oS", [2, 6144], F32)
    wb = [P.sb("wb%d" % i, [128, 16, 512], F32) for i in range(2)]
    banks = [P.ps("bank%d" % i, [128, 512], F32) for i in range(2)]
    P.dma("sp", cS.ap, cT.ap, [cT], [cS], cS)
    P.dma("sp", bS.ap, bias.ap, [bias], [bS], bS)
    P.act(cS.ap, cS.ap, AF.Silu, [cS], [cS])
    w3 = w.ap.rearrange("(c p) n -> p c n", p=128)

    def load(nb):
        P.dma("sp", wb[nb % 2].ap, w3[:, :, nb * 512:(nb + 1) * 512], [w], [wb[nb % 2]], wb[nb % 2])

    load(0)
    load(1)
    for nb in range(12):
        pp = banks[nb % 2]
        for c in range(16):
            P.mm(pp[0:2, :], cS[:, c, :], wb[nb % 2][:, c, :], c == 0, c == 15, [cS, wb[nb % 2]], [pp], signal=(c == 15))
        P.tt("dve", oS[:, nb * 512:(nb + 1) * 512], pp[0:2, :], bS[:, nb * 512:(nb + 1) * 512], ALU.add, [pp, bS], [oS])
        if nb + 2 < 12:
            load(nb + 2)
    P.dma("sp", mod.ap, oS.ap, [oS], [mod], oS)
    return P.finish()
import math

BIGIDX = 1 << 24
ARENA_WORDS = 46080
LAMBDA_INIT = {1: 0.8 - 0.6 * math.exp(-0.3 * 1), 3: 0.8 - 0.6 * math.exp(-0.3 * 3)}


def build_fused(nlayers=4, debug_out=False):
    P = Prog()
    nc = P.nc
    IN = lambda n, s, d=F32: P.dram(n, s, d, "ExternalInput")
    x_in = IN("x", [T, D])
    cT = IN("cT", [128, 16, 1])
    ada_wq = IN("ada_wq", [4, D, 3072])
    ada_bq = IN("ada_bq", [1, 12288])
    ngT_d = IN("ngT", [128, 4, 4, 16])
    ngrows = IN("ngrows", [16, D])
    ab_w_in = IN("ab_w_in", [2, D, 3584])
    ab_w_out = IN("ab_w_out", [2, 1024, D])
    sinkb_d = IN("sinkb", [128, 2, 4])
    c_w_in = IN("c_w_in", [2, D, 6144])
    c_w_out = IN("c_w_out", [2, D, D])
    lamb_d = IN("lamb", [128, 2, 4, 128])
    sublnb_d = IN("sublnb", [128, 2, 256])
    lin_d = IN("lin", [128, 2, 2])
    w_up = IN("w_up", [4, D, 2 * DFF])
    w_down = IN("w_down", [4, DFF, D])
    convT_d = IN("convT", [128, 4, 4, NFC])
    Ld = {tp: IN("L" + tp, [128, TYPES[tp][2] - TYPES[tp][3] + 512]) for tp in TYPES}
    Lc_d = IN("Lc", [128, 4992])
    ident_d = IN("ident", [128, 128], BF16)
    identf_d = IN("identf", [128, 128])
    flags_d = IN("flags", [128, 2])
    idxkv_d = IN("idxkv", [128, 43], mybir.dt.int32)
    out = P.dram("out", [T, D], F32, "ExternalOutput")

    mown = P.dram("mown", [12, 1024], F32)
    mall = P.dram("mall", [48, 1024], F32)
    qown = P.dram("qown", [16, 128, T], BF16)
    k_ab_own = P.dram("k_ab_own", [768, 1024], BF16)
    k_ab_all = P.dram("k_ab_all", [3072, 1024], BF16)
    v_ab_own = P.dram("v_ab_own", [1024, 774], BF16)
    v_ab_all = P.dram("v_ab_all", [4096, 774], BF16)
    kv_c_own = P.dram("kv_c_own", [4096, 1024], BF16)
    kv_c_all = P.dram("kv_c_all", [4 * 4096, 1024], BF16)
    kvc_chunks = []
    for i in range(8):
        cb = Buf("kvc_chunk%d" % i, kv_c_all.t, kv_c_all.ap[2048 * i:2048 * (i + 1), :])
        kvc_chunks.append(cb)
    kvo_chunks = [Buf("kvo_chunk%d" % i, kv_c_own.t, kv_c_own.ap[512 * i:512 * (i + 1), :]) for i in range(8)]
    xmid = P.dram("xmid", [T, D], F32)
    xnext = [P.dram("xnext%d" % i, [T, D], F32) for i in range(2)]
    yscr = P.dram("yscr", [T, D], F32)
    hown = P.dram("hown", [2, D], F32)
    hall = P.dram("hall", [8, D], F32)

    def PS(name, shape, dt):
        b = P.sb(name, shape, dt)
        b.persistent = True
        return b
    ident = PS("ident_s", [128, 128], BF16)
    identf = PS("identf_s", [128, 128], F32)
    modTall = PS("modTall", [128, 4, 96], F32)
    ngS = PS("ngS", [128, 4, 4, 16], F32)
    stat = [PS("stat%d" % i, [128, 16], F32) for i in range(4)]
    sm = PS("small", [128, 64], F32)
    GT = PS("GT", [128, 16], F32)
    SHT = PS("SHT", [128, 16], F32)
    flagS = PS("flagS", [128, 2], F32)
    idxkv = PS("idxkv_s", [128, 43], mybir.dt.int32)
    banks = [P.ps("bank%d" % i, [128, 512], F32) for i in range(8)]

    def bfview(b):
        v = Buf(b.name + "v", b.t, b.ap.bitcast(BF16).rearrange("p (a b) -> p a b", b=128))
        v.w = b.w
        v.r = b.r
        return v
    bankv = [bfview(b) for b in banks]
    A = Arena(P, "arena", ARENA_WORDS)

    def stage_begin():
        P.barrier()
        P.release_stage_sems()
        A.reset(0)
        for b in banks:
            b.w.clear()
            b.r.clear()

    for (s, d) in ((ident, ident_d), (identf, identf_d), (ngS, ngT_d), (flagS, flags_d), (idxkv, idxkv_d)):
        P.dma("sp", s.ap, d.ap, [d], [s], s)
    for st_ in stat:
        P.memset("dve", st_.ap, 0.0, [st_])
    P.memset("dve", sm.ap, 0.0, [sm])
    regs = {}

    def init_regs(e):
        ins = None
        for nm, val in (("k", 3071), ("v", 4095), ("h", 7)):
            regs[nm] = e.alloc_register("bnd_" + nm)
            ins = e.reg_mov(regs[nm], val)
        return ins
    P.ops["pool"].append(("op", init_regs, None, 0))

    cS = A.take("cS", [128, 16, 1], F32)
    bS = A.take("bS", [1, 12288], F32)
    oS = A.take("oS", [1, 12288], F32)
    NWM = 4
    wbm = [A.take("wbm", [128, 16, 512], BF16) for _ in range(NWM)]
    cSb = A.take("cSb", [128, 16, 2], BF16)
    P.dma("sp", cS.ap, cT.ap, [cT], [cS], cS)
    P.dma("sp", bS[0:1, :], ada_bq.ap, [ada_bq], [bS], bS)
    P.act(cSb[:, :, 0:1], cS.ap, AF.Silu, [cS], [cSb])
    blks = [(l, nb) for l in range(4) for nb in range(6)]

    def load_m(i):
        l, nb = blks[i]
        P.dma("pool", wbm[i % NWM].ap, ada_wq.ap[l].rearrange("(c p) n -> p c n", p=128)[:, :, nb * 512:(nb + 1) * 512],
              [ada_wq], [wbm[i % NWM]], wbm[i % NWM])
    for i_ in range(NWM - 1):
        load_m(i_)
    for i, (l, nb) in enumerate(blks):
        pp = banks[i % 2]
        for c in range(16):
            P.mm(pp[0:1, :], cSb[:, c, 0:1], wbm[i % NWM][:, c, :], c == 0, c == 15, [cSb, wbm[i % NWM]], [pp], signal=(c == 15))
        o = l * 3072 + nb * 512
        P.tt("dve", oS[0:1, o:o + 512], pp[0:1, :], bS[0:1, o:o + 512], ALU.add, [pp, bS], [oS])
        if i + NWM - 1 < len(blks):
            load_m(i + NWM - 1)
    P.dma("sp", mown.ap.rearrange("a b -> (a b)").rearrange("(o n) -> o n", o=1), oS[0:1, :], [oS], [mown], oS)
    P.op_sem("pool", lambda e: e.collective_compute("AllGather", ALU.bypass, replica_groups=[[0, 1, 2, 3], [4, 5, 6, 7]],
                                                   ins=[mown.ap], outs=[mall.ap]), [mown], [mall], "cc", 1)
    mflat = mall.ap.rearrange("a b -> (a b)")
    mrow = A.take("mrow", [96, 128], F32)
    for l in range(4):
        for r in range(4):
            o = (r * 4 + l) * 3072
            P.dma("sp", mrow[24 * r:24 * r + 24, :], mflat[o:o + 3072].rearrange("(a b) -> a b", b=128), [mall], [mrow], mrow)
        P.tr(banks[2][:, 0:96], mrow[0:96, :], identf[0:96, 0:96], [mrow, identf], [banks[2]])
        P.copy("dve", modTall[:, l, :], banks[2][:, 0:96], [banks[2]], [modTall])

    def grow_ap(l, which):
        r = 1 if which == 0 else 3
        o = (r * 4 + l) * 3072 + 1024
        return mflat[o:o + 2048].rearrange("(o n) -> o n", o=1).partition_broadcast(128)

    def stage_LA(kind, layer, xcur):
        j = layer // 2
        stage_begin()
        modT = Buf("modTv", modTall.t, modTall[:, layer, :].rearrange("p (a b) -> p a b", b=16))
        modT.w, modT.r = modTall.w, modTall.r
        ngv = Buf("ngv", ngS.t, ngS[:, layer, :, :])
        ngv.w, ngv.r = ngS.w, ngS.r
        emit_mod_prep(P, modT, ngv, 0, 1, 0, GT, SHT)
        hT = A.take("hT", [128, DC, T], BF16)
        xb = [A.take("xb", [128, D], F32) for _ in range(2)]
        xnb = [A.take("xnb", [128, D], BF16) for _ in range(2)]
        junk = A.take("junk", [128, D], BF16)
        NWB = 7
        wb = [A.take("wb", [128, DC, 512], BF16) for _ in range(NWB)]
        stg_f = [A.take("stgf", [128, T], BF16) for _ in range(2)]
        if kind == "ab":
            stg_t = [A.take("stgt", [128, 4, 129], BF16) for _ in range(3)]
            for s in stg_t:
                P.memset("pool", s[:, :, 128:129], 1.0, [s])
            w_in = ab_w_in
            W = 3584
            kvo = k_ab_own
            vvo = v_ab_own
            kview = kvo.ap.rearrange("(h p) t -> h p t", p=128)
            vview = vvo.ap.rearrange("t (h n) -> t h n", n=129)
            blocks = [
                (0, [(0, ("q", 0)), (128, ("q", 1)), (256, ("q", 2)), (384, ("q", 3))], []),
                (512, [(0, ("k", 0)), (128, ("k", 1))], [(256, 2, 0)]),
                (1024, [(i * 128, ("q", 4 + i)) for i in range(4)], []),
                (1536, [(i * 128, ("q", 8 + i)) for i in range(4)], []),
                (2048, [(i * 128, ("q", 12 + i)) for i in range(4)], []),
                (2560, [(i * 128, ("k", 2 + i)) for i in range(4)], []),
                (3072, [], [(0, 4, 2)]),
            ]
        else:
            stg_t = [A.take("stgt", [128, 512], BF16) for _ in range(3)]
            w_in = c_w_in
            W = 6144
            kvo = kv_c_own
            vvo = kv_c_own
            kview = kvo.ap[0:2048, :].rearrange("(h p) t -> h p t", p=128)
            vviews = [kvo.ap[2048 + 512 * i_:2048 + 512 * (i_ + 1), :].rearrange("q (a c) -> (q a) c", a=2) for i_ in range(4)]
            vview = None
            blocks = []
            for i in range(4):
                blocks.append((i * 512, [(jj * 128, ("q", i * 4 + jj)) for jj in range(4)], []))
            for i in range(4):
                blocks.append((2048 + i * 512, [(jj * 128, ("k", i * 4 + jj)) for jj in range(4)], []))
            for i in range(4):
                blocks.append((4096 + i * 512, [], [(0, 512, i * 512)]))
        w3 = w_in.ap[j].rearrange("(c p) n -> p c n", p=128)

        def load_w(bi):
            c0 = blocks[bi][0]
            b = wb[bi % NWB]
            P.dma("pool", b.ap, w3[:, :, c0:c0 + 512], [w_in], [b], b)
        GRP = [[0, 1, 2, 3], [4, 5, 6, 7]]

        def gather(src_, dst_, r0, nr):
            wb_ = kvc_chunks[r0 // 512] if dst_ is kv_c_all else dst_
            rb_ = kvo_chunks[r0 // 512] if src_ is kv_c_own else src_
            P.op_sem("pool", lambda e: e.collective_compute(
                "AllGather", ALU.bypass, replica_groups=GRP,
                ins=[src_.ap[r0:r0 + nr, :]], outs=[dst_.ap[4 * r0:4 * r0 + 4 * nr, :]]), [rb_], [wb_], "cc", 1)
        if kind == "ab":
            order = [1, 5, 6, 0, 2, 3, 4]
            after = {5: [(k_ab_own, k_ab_all, 0, 384), (k_ab_own, k_ab_all, 384, 384)],
                     6: [(v_ab_own, v_ab_all, 0, 512), (v_ab_own, v_ab_all, 512, 512)]}
        else:
            order = [8, 9, 10, 11, 4, 5, 6, 7, 0, 1, 2, 3]
            after = {}
            for i in range(4):
                after[8 + i] = [(kv_c_own, kv_c_all, 512 * (4 + i), 512)]
                after[4 + i] = [(kv_c_own, kv_c_all, 512 * i, 512)]
        blocks_o = [blocks[i] for i in order]
        after_o = [after.get(i, []) for i in order]
        blocks = blocks_o
        for i_ in range(min(NWB - 1, len(blocks))):
            load_w(i_)
        pst = [bankv[6], bankv[7]]
        for t in range(NT):
            emit_norm_T(P, xcur, t * 128, 128, hT, t * 128, GT, SHT, ident, xb, xnb, junk, stat, pst, t)
        psm = banks[0:4]
        pi = fi = ti = 0
        for bi, (c0, funits, tranges) in enumerate(blocks):
            if bi + NWB - 1 < len(blocks):
                load_w(bi + NWB - 1)
            b = wb[bi % NWB]
            for (off, (which, idx)) in funits:
                sg = stg_f[fi % 2]
                fi += 1
                for half in range(2):
                    pp = psm[pi % 4]
                    pi += 1
                    for c in range(DC):
                        P.mm(pp.ap, b[:, c, off:off + 128], hT[:, c, half * 512:(half + 1) * 512],
                             c == 0, c == DC - 1, [b, hT], [pp], signal=(c == DC - 1))
                    dst = sg[:, half * 512:(half + 1) * 512]
                    sc = SCALE if which == "q" else 1.0
                    if half == 0:
                        P.act(dst, pp.ap, AF.Copy, [pp], [sg], scale=sc)
                    else:
                        P.ts("dve", dst, pp.ap, sc, None, ALU.mult, None, [pp], [sg])
                if which == "q":
                    P.dma("sp", qown.ap[idx], sg.ap, [sg], [qown], sg)
                else:
                    P.dma("sp", kview[idx], sg.ap, [sg], [kvo if kind == "ab" else kvo_chunks[idx // 4]], sg)
            for tr_ in tranges:
                for t in range(NT):
                    pp = psm[pi % 4]
                    pi += 1
                    sg = stg_t[ti % 3]
                    ti += 1
                    if kind == "ab":
                        off, nh, h0 = tr_
                        wdt = nh * 128
                    else:
                        off, wdt, dcol = tr_
                    for c in range(DC):
                        P.mm(pp[:, 0:wdt], hT[:, c, t * 128:(t + 1) * 128], b[:, c, off:off + wdt],
                             c == 0, c == DC - 1, [b, hT], [pp], signal=(c == DC - 1))
                    if kind == "ab":
                        src = pp[:, 0:wdt].rearrange("p (h n) -> p h n", n=128)
                        dsts = sg[:, 0:nh, 0:128]
                    else:
                        src = pp[:, 0:wdt]
                        dsts = sg[:, 0:wdt]
                    if t % 2 == 0:
                        P.act(dsts, src, AF.Copy, [pp], [sg])
                    else:
                        P.copy("dve", dsts, src, [pp], [sg])
                    if kind == "ab":
                        P.dma("sp", vview[t * 128:(t + 1) * 128, h0:h0 + nh, :], sg[:, 0:nh, :], [sg], [vvo], sg)
                    else:
                        P.dma("sp", vviews[dcol // 512][t * 128:(t + 1) * 128, :], sg[:, 0:wdt], [sg], [kvo_chunks[4 + dcol // 512]], sg)
            for g_ in after_o[bi]:
                gather(*g_)

    LAG = 3
    NSB = 4

    def attn_blocks(jobs, vw, obank, sbank, tmp, PT, cnt):
        n = len(jobs)
        for i in range(n + LAG):
            if i < n:
                kap, qap, vap, lap, slope, Rl = jobs[i]
                S = sbank[(cnt + i) % NSB]
                tm = tmp[(cnt + i) % NSB]
                pt = PT[(cnt + i) % NSB]
                P.mm(S.ap, kap, qap, True, True, Rl, [S])
                P.stt("dve", tm.ap, lap, -slope, S.ap, ALU.mult, ALU.add, [S] + Rl, [tm])
                P.act(pt.ap, tm.ap, AF.Exp, [tm], [pt])
            if i >= LAG:
                ii = i - LAG
                kap, qap, vap, lap, slope, Rl = jobs[ii]
                pt = PT[(cnt + ii) % NSB]
                for qs in range(4):
                    P.mm(obank[qs][:, 0:vw], pt[:, qs * 128:(qs + 1) * 128], vap,
                         ii == 0, ii == n - 1, [pt] + Rl, [obank[qs]], signal=(qs == 3))
        return cnt + n

    def stage_LB(kind, layer, xcur):
        j = layer // 2
        ab = kind == "ab"
        KC = 8 if ab else 16
        VW = 129 if ab else 257
        stage_begin()
        oT = A.take("oT", [128, KC, T], BF16)
        tmp = [A.take("tmp", [128, 512], F32) for _ in range(4)]
        PT = [A.take("PT", [128, 512], BF16) for _ in range(4)]
        on = [A.take("on", [128, 4, VW - 1], BF16) for _ in range(2)]
        mark = A.off
        sbank = [banks[0], banks[1], banks[7], banks[6]]
        obank = banks[2:6]
        tbv = bankv[6]
        cnt = 0
        if ab:
            qa = A.take("qall", [128, 16, T], BF16)
            ka = A.take("kall", [128, 6, 3072], BF16)
            va = A.take("vall", [128, 24, 774], BF16)
            Ls = {tp: A.take("Ls" + tp, [128, TYPES[tp][2] - TYPES[tp][3] + 512], F32) for tp in TYPES}
            P.dma("sp", qa.ap, qown.ap.rearrange("h p t -> p h t"), [qown], [qa], qa)
            for tp in TYPES:
                P.dma("sp", Ls[tp].ap, Ld[tp].ap, [Ld[tp]], [Ls[tp]], Ls[tp])
            kviews = [[Buf("kv_%d_%d" % (hd, s_), ka.t, ka[:, hd, s_ * 1024:(s_ + 1) * 1024]) for s_ in range(3)] for hd in range(6)]
            vviews_ = [Buf("vv_%d" % wk, va.t, va[:, wk, :]) for wk in range(24)]
            allk = [b_ for row_ in kviews for b_ in row_]
            P.memset("dve", ka.ap.rearrange("p a b -> p (a b)"), 0.0, allk)
            P.memset("dve", va.ap.rearrange("p a b -> p (a b)"), 0.0, vviews_)

            def gk(hd, s_):
                col = s_ * 6 + hd
                kb_ = kviews[hd][s_]
                P.op_sem("pool", lambda e: e.indirect_dma_start(
                    out=kb_.ap, out_offset=None, in_=k_ab_all.ap[:, :],
                    in_offset=bass.IndirectOffsetOnAxis(ap=idxkv[:, col:col + 1], axis=0),
                    bounds_check=regs["k"], oob_is_err=False), [k_ab_all, idxkv], [kb_], P.semkey_for(kb_), 16)

            def gv(wk):
                col = 18 + wk
                vb_ = vviews_[wk]
                P.op_sem("pool", lambda e: e.indirect_dma_start(
                    out=vb_.ap, out_offset=None, in_=v_ab_all.ap[:, :],
                    in_offset=bass.IndirectOffsetOnAxis(ap=idxkv[:, col:col + 1], axis=0),
                    bounds_check=regs["v"], oob_is_err=False), [v_ab_all, idxkv], [vb_], P.semkey_for(vb_), 16)
            for s_ in range(3):
                gk(0, s_)
            for wk in range(7, 17):
                gv(wk)
            for hd in range(1, 6):
                for s_ in range(3):
                    gk(hd, s_)
            for wk in list(range(0, 7)) + list(range(17, 24)):
                gv(wk)
            P.dma("sp", sm[:, 0:4], sinkb_d.ap[:, j, :], [sinkb_d], [sm], sm)
            P.act(sm[:, 4:8], sm[:, 0:4], AF.Exp, [sm], [sm])
            outs = []
            for u in range(4):
                outs.append((u, [(u, "A", SLOPES16[u])], u // 2, u))
            for i in range(4):
                outs.append((4 + i, [(4 + i, "g0", SLOPES16[4 + i]), (8 + i, "g1", SLOPES16[8 + i]),
                                     (12 + i, "g2", SLOPES16[12 + i])], 2 + i, None))
            fin = 0
            for (ou, qlist, kv, sink) in outs:
                for qb in range(2):
                    q0 = qb * 512
                    jobs = []
                    for (qu, tp, slope) in qlist:
                        R_, dil, dmax, dmin = TYPES[tp]
                        for dl in range(dmax, dmin - 1, -128):
                            wk = (q0 - dl + 1024) // 128
                            jobs.append((ka[:, kv, wk * 128:(wk + 1) * 128], qa[:, qu, q0:q0 + 512],
                                         va[:, wk, kv * 129:(kv + 1) * 129], Ls[tp][:, dl - dmin:dl - dmin + 512],
                                         slope, [kviews[kv][wk // 8], qa, vviews_[wk], Ls[tp]]))
                    cnt = attn_blocks(jobs, 129, obank, sbank, tmp, PT, cnt)
                    onb = on[fin % 2]
                    fin += 1
                    for qs in range(4):
                        dn = sm[:, 8 + qs:9 + qs]
                        if sink is not None:
                            P.tt("dve", dn, obank[qs][:, 128:129], sm[:, 4 + sink:5 + sink], ALU.add, [obank[qs], sm], [sm])
                        else:
                            P.copy("dve", dn, obank[qs][:, 128:129], [obank[qs]], [sm])
                        P.op("dve", lambda e, dn=dn: e.reciprocal(out=dn, in_=dn), [sm], [sm])
                        P.ts("dve", onb[:, qs, :], obank[qs][:, 0:128], dn, None, ALU.mult, None, [obank[qs], sm], [onb])
                    for qs in range(4):
                        P.tr(tbv[:, qs, :], onb[:, qs, :], ident.ap, [onb, ident], [tbv], signal=(qs == 3))
                    P.act(oT[:, ou, q0:q0 + 512], tbv.ap[:, 0:4, :].rearrange("p a b -> p (a b)"), AF.Copy, [tbv], [oT])
            w_out = ab_w_out
        else:
            kh = [A.take("kh", [128, 2, 4096], BF16) for _ in range(2)]
            vh = [A.take("vh", [128, 32, 257], BF16) for _ in range(2)]
            qh = [A.take("qh", [128, 2, T], BF16) for _ in range(2)]
            Lc = A.take("Lc_s", [128, 4992], F32)
            o1 = A.take("o1", [128, 4, 257], F32)
            of = A.take("of", [128, 256], F32)
            jk = A.take("jk", [128, 256], BF16)
            lam = A.take("lam_s", [128, 4, 128], F32)
            SG = A.take("SG", [128, 256], F32)
            P.dma("sp", Lc.ap, Lc_d.ap, [Lc_d], [Lc], Lc)
            P.dma("sp", lam.ap, lamb_d.ap[:, j, :, :], [lamb_d], [lam], lam)
            P.dma("sp", SG.ap, sublnb_d.ap[:, j, :], [sublnb_d], [SG], SG)
            P.dma("sp", sm[:, 0:2], lin_d.ap[:, j, :], [lin_d], [sm], sm)
            for jj in range(2):
                P.tt("dve", lam[:, 2 * jj, :], lam[:, 2 * jj, :], lam[:, 2 * jj + 1, :], ALU.mult, [lam], [lam])
                P.op("dve", lambda e, jj=jj: e.tensor_reduce(out=sm[:, 4 + jj:5 + jj], in_=lam[:, 2 * jj, :], axis=AX.X, op=ALU.add),
                     [lam], [sm])
            P.act(sm[:, 4:6], sm[:, 4:6], AF.Exp, [sm], [sm])
            P.tt("dve", sm[:, 2:3], sm[:, 4:5], sm[:, 5:6], ALU.subtract, [sm], [sm])
            P.tt("dve", sm[:, 2:3], sm[:, 2:3], sm[:, 0:1], ALU.add, [sm], [sm])
            P.ts("dve", sm[:, 3:4], sm[:, 2:3], -1.0, None, ALU.mult, None, [sm], [sm])
            P.ts("dve", SG.ap, SG.ap, sm[:, 1:2], None, ALU.mult, None, [SG, sm], [SG])
            kall5 = kv_c_all.ap.rearrange("(c r x) t -> c r x t", c=8, r=4)
            for i in range(2):
                P.memset("pool", vh[i][:, :, 256:257], 1.0, [vh[i]])

            def load_head(h):
                kb, vb, qb_ = kh[h % 2], vh[h % 2], qh[h % 2]
                ko = ((2 * h) % 4) * 128
                for r in range(4):
                    P.dma("sp", kb[:, :, r * 1024:(r + 1) * 1024],
                          kall5[h // 2, r, ko:ko + 256, :].rearrange("(j p) t -> p j t", p=128), [kvc_chunks[h // 2]], [kb], kb)
                    vsrc = kall5[4 + h // 2, r].rearrange("q (a c) -> (q a) c", a=2).rearrange("(k p) c -> p k c", p=128)
                    vo = (h % 2) * 256
                    P.dma("sp", vb[:, r * 8:(r + 1) * 8, 0:256], vsrc[:, :, vo:vo + 256], [kvc_chunks[4 + h // 2]], [vb], vb)
                P.dma("sp", qb_.ap, qown.ap[2 * h:2 * h + 2].rearrange("j p t -> p j t"), [qown], [qb_], qb_)
            load_head(0)
            fin = 0
            for h in range(8):
                if h + 1 < 8:
                    load_head(h + 1)
                khb, vhb, qhb = kh[h % 2], vh[h % 2], qh[h % 2]
                for qb in range(2):
                    q0 = qb * 512
                    for jj in range(2):
                        jobs = []
                        for kt in range(32):
                            c0 = q0 - kt * 128 + 3968
                            jobs.append((khb[:, jj, kt * 128:(kt + 1) * 128], qhb[:, jj, q0:q0 + 512], vhb[:, kt, :],
                                         Lc[:, c0:c0 + 512], SLOPES8[h], [khb, qhb, vhb, Lc]))
                        cnt = attn_blocks(jobs, 257, obank, sbank, tmp, PT, cnt)
                        if jj == 0:
                            for qs in range(4):
                                P.act(o1[:, qs, :], obank[qs][:, 0:257], AF.Copy, [obank[qs]], [o1])
                    onb = on[fin % 2]
                    fin += 1
                    for qs in range(4):
                        r1 = sm[:, 8 + 4 * qs:9 + 4 * qs]
                        r2 = sm[:, 9 + 4 * qs:10 + 4 * qs]
                        ss = sm[:, 10 + 4 * qs:11 + 4 * qs]
                        rs = sm[:, 11 + 4 * qs:12 + 4 * qs]
                        P.op("dve", lambda e, r1=r1, qs=qs: e.reciprocal(out=r1, in_=o1[:, qs, 256:257]), [o1], [sm])
                        P.op("dve", lambda e, r2=r2, qs=qs: e.reciprocal(out=r2, in_=obank[qs][:, 256:257]), [obank[qs]], [sm])
                        P.tt("dve", r2, r2, sm[:, 3:4], ALU.mult, [sm], [sm])
                        P.ts("dve", of.ap, o1[:, qs, 0:256], r1, None, ALU.mult, None, [o1, sm], [of])
                        P.stt("dve", of.ap, obank[qs][:, 0:256], r2, of.ap, ALU.mult, ALU.add, [obank[qs], sm, of], [of])
                        P.memset("dve", ss, 0.0, [sm])
                        P.act(jk.ap, of.ap, AF.Square, [of], [jk, sm], accum_out=ss)
                        P.ts("dve", rs, ss, 1.0 / 256, EPS, ALU.mult, ALU.add, [sm], [sm])
                        P.act(rs, rs, AF.Sqrt, [sm], [sm])
                        P.op("dve", lambda e, rs=rs: e.reciprocal(out=rs, in_=rs), [sm], [sm])
                        P.stt("dve", onb[:, qs, :], of.ap, rs, SG.ap, ALU.mult, ALU.mult, [of, sm, SG], [onb])
                    for qs in range(4):
                        for i in range(2):
                            P.tr(tbv[:, qs * 2 + i, :], onb[:, qs, i * 128:(i + 1) * 128], ident.ap, [onb, ident], [tbv],
                                 signal=(qs == 3 and i == 1))
                    tv = tbv.ap.rearrange("p (a i) b -> p i a b", i=2)
                    for i in range(2):
                        P.act(oT[:, 2 * h + i, q0:q0 + 512].rearrange("p (a b) -> p a b", b=128), tv[:, i, :, :], AF.Copy, [tbv], [oT])
            w_out = c_w_out

        P.barrier()
        A.reset(mark)
        for b in banks:
            b.w.clear()
            b.r.clear()
        wo = [A.take("wo", [128, KC, 512], BF16) for _ in range(4)]
        ggb_ = A.take("GGb", [128, D], F32)
        xt2 = [A.take("xt2", [128, D], F32) for _ in range(2)]
        yo = [A.take("yo", [128, D], F32) for _ in range(2)]
        junk2 = A.take("junk2", [128, 512], BF16)
        wo3 = w_out.ap[j].rearrange("(c p) n -> p c n", p=128)
        for nb in range(4):
            P.dma("pool", wo[nb].ap, wo3[:, :, nb * 512:(nb + 1) * 512], [w_out], [wo[nb]], wo[nb])
        P.dma("sp", ggb_.ap, ngrows.ap[layer * 4 + 1:layer * 4 + 2, :].partition_broadcast(128), [ngrows], [ggb_], ggb_)
        P.dma("sp", yo[0].ap, grow_ap(layer, 0), [mall], [yo[0]], yo[0])
        P.tt("pool", ggb_.ap, ggb_.ap, yo[0].ap, ALU.mult, [ggb_, yo[0]], [ggb_])
        for t in range(NT):
            bs = banks[0:4] if t % 2 == 0 else banks[4:8]
            xx = xt2[t % 2]
            yy = yo[t % 2]
            P.dma("sp", xx.ap, xcur.ap[t * 128:(t + 1) * 128, :], [xcur], [xx], xx)
            for nb in range(4):
                for c in range(KC):
                    P.mm(bs[nb].ap, oT[:, c, t * 128:(t + 1) * 128], wo[nb][:, c, :], c == 0, c == KC - 1, [oT, wo[nb]], [bs[nb]],
                         signal=(c == KC - 1))
            st = stat[t % 4]
            so = 8
            for nb in range(4):
                P.act(junk2.ap, bs[nb].ap, AF.Square, [bs[nb]], [junk2, st], accum_out=st[:, so + nb:so + nb + 1])
            ss = st[:, so + 4:so + 5]
            rs = st[:, so + 5:so + 6]
            P.op("dve", lambda e, ss=ss, st=st: e.tensor_reduce(out=ss, in_=st[:, 8:12], axis=AX.X, op=ALU.add), [st], [st])
            P.ts("dve", rs, ss, 1.0 / D, EPS, ALU.mult, ALU.add, [st], [st])
            P.act(rs, rs, AF.Sqrt, [st], [st])
            P.op("dve", lambda e, rs=rs: e.reciprocal(out=rs, in_=rs), [st], [st])
            for nb in range(4):
                P.stt("dve", yy[:, nb * 512:(nb + 1) * 512], bs[nb].ap, rs, ggb_[:, nb * 512:(nb + 1) * 512], ALU.mult, ALU.mult,
                      [bs[nb], st, ggb_], [yy])
            P.tt("dve", yy.ap, yy.ap, xx.ap, ALU.add, [yy, xx], [yy])
            P.dma("sp", xmid.ap[t * 128:(t + 1) * 128, :], yy.ap, [yy], [xmid], yy)
            if t == 0:
                P.dma("sp", hown.ap[0:1, :], yy[0:1, :], [yy], [hown], yy)
            if t == NT - 1:
                P.dma("sp", hown.ap[1:2, :], yy[127:128, :], [yy], [hown], yy)
        P.op_sem("pool", lambda e: e.collective_compute("AllGather", ALU.bypass, replica_groups=[[0, 1, 2, 3], [4, 5, 6, 7]],
                                                       ins=[hown.ap], outs=[hall.ap]), [hown], [hall], "cc", 1)

    def stage_LC(layer, xdst):
        stage_begin()
        modT = Buf("modTv", modTall.t, modTall[:, layer, :].rearrange("p (a b) -> p a b", b=16))
        modT.w, modT.r = modTall.w, modTall.r
        ngv = Buf("ngv", ngS.t, ngS[:, layer, :, :])
        ngv.w, ngv.r = ngS.w, ngS.r
        emit_mod_prep(P, modT, ngv, 3, 4, 2, GT, SHT)
        aT = A.take("aT", [128, NFC, T], BF16)
        convS = A.take("convS", [128, 4, NFC], F32)
        base = A.off
        P.dma("sp", convS.ap, convT_d.ap[:, layer, :, :], [convT_d], [convS], convS)
        h2T = A.take("h2T", [128, DC, T + 2], BF16)
        wg = [A.take("wg", [128, DC, 256], BF16) for _ in range(2)]
        wu = [A.take("wu", [128, DC, 256], BF16) for _ in range(2)]
        mark = A.off
        xb = [A.take("xb", [128, D], F32) for _ in range(2)]
        xnb = [A.take("xnb", [128, D], BF16) for _ in range(2)]
        junk = A.take("junk", [128, D], BF16)
        w3 = w_up.ap[layer].rearrange("(c p) n -> p c n", p=128)
        NG = (NFC + 1) // 2

        def load_wu(g):
            c0 = g * 256
            wd_ = min(256, DFF - c0)
            P.dma("pool", wg[g % 2][:, :, 0:wd_], w3[:, :, c0:c0 + wd_], [w_up], [wg[g % 2]], wg[g % 2])
            P.dma("pool", wu[g % 2][:, :, 0:wd_], w3[:, :, DFF + c0:DFF + c0 + wd_], [w_up], [wu[g % 2]], wu[g % 2])
        load_wu(0)
        load_wu(1)
        pst = [bankv[6], bankv[7]]
        for t in range(NT):
            emit_norm_T(P, xmid, t * 128, 128, h2T, t * 128, GT, SHT, ident, xb, xnb, junk, stat, pst, t)
        xt = xb[NT % 2]
        P.memset("dve", xt[0:2, :], 0.0, [xt])
        P.op_sem("pool", lambda e: e.indirect_dma_start(
            out=xt[:, :], out_offset=None, in_=hall.ap[:, :],
            in_offset=bass.IndirectOffsetOnAxis(ap=idxkv[:, 42:43], axis=0), bounds_check=regs["h"], oob_is_err=False),
            [hall, idxkv], [xt], P.semkey_for(xt), 16)
        emit_norm_T(P, None, 0, 2, h2T, T, GT, SHT, ident, xb, xnb, junk, stat, pst, NT)
        P.barrier()
        for b in banks:
            b.w.clear()
            b.r.clear()
        A.reset(mark)
        gsb = [A.take("gsb", [128, T + 4], F32) for _ in range(2)]
        acc = [A.take("acc", [128, T], F32) for _ in range(2)]
        usb = [A.take("usb", [128, T], F32) for _ in range(2)]
        sets = [(banks[0], banks[1]), (banks[2], banks[3]), (banks[4], banks[5])]
        si = 0
        ph = banks[6]
        for fc in range(NFC):
            g = fc // 2
            off = (fc % 2) * 128
            if fc % 2 == 0 and g >= 1 and g + 1 < NG:
                load_wu(g + 1)
            wgb, wub = wg[g % 2], wu[g % 2]
            gs, ac, us = gsb[fc % 2], acc[fc % 2], usb[fc % 2]
            hc = (fc % 8) * 2
            sg = sets[si % 3]
            si += 1
            for half in range(2):
                for c in range(DC):
                    P.mm(sg[half].ap, wgb[:, c, off:off + 128], h2T[:, c, half * 512:(half + 1) * 512],
                         c == 0, c == DC - 1, [wgb, h2T], [sg[half]], signal=(c == DC - 1))
            for c in range(DC):
                P.mm(ph[:, hc:hc + 2], wgb[:, c, off:off + 128], h2T[:, c, T:T + 2],
                     c == 0, c == DC - 1, [wgb, h2T], [ph], signal=(c == DC - 1))
            P.act(gs[:, 1:513], sg[0].ap, AF.Copy, [sg[0]], [gs])
            P.act(gs[:, 513:1025], sg[1].ap, AF.Copy, [sg[1]], [gs])
            P.tt("dve", gs[:, 0:1], ph[:, hc:hc + 1], flagS[:, 0:1], ALU.mult, [ph, flagS], [gs])
            P.tt("dve", gs[:, 1025:1026], ph[:, hc + 1:hc + 2], flagS[:, 1:2], ALU.mult, [ph, flagS], [gs])
            su = sets[si % 3]
            si += 1
            for half in range(2):
                for c in range(DC):
                    P.mm(su[half].ap, wub[:, c, off:off + 128], h2T[:, c, half * 512:(half + 1) * 512],
                         c == 0, c == DC - 1, [wub, h2T], [su[half]], signal=(c == DC - 1))
            P.act(us[:, 0:512], su[0].ap, AF.Copy, [su[0]], [us])
            P.act(us[:, 512:1024], su[1].ap, AF.Copy, [su[1]], [us])
            P.ts("dve", ac.ap, gs[:, 0:T], convS[:, 0, fc:fc + 1], None, ALU.mult, None, [gs, convS], [ac])
            P.stt("dve", ac.ap, gs[:, 1:T + 1], convS[:, 1, fc:fc + 1], ac.ap, ALU.mult, ALU.add, [gs, convS, ac], [ac])
            P.stt("dve", ac.ap, gs[:, 2:T + 2], convS[:, 2, fc:fc + 1], ac.ap, ALU.mult, ALU.add, [gs, convS, ac], [ac])
            P.act(ac.ap, ac.ap, AF.Gelu_apprx_tanh, [ac, convS], [ac], bias=convS[:, 3, fc:fc + 1])
            P.tt("pool", aT[:, fc, :], ac.ap, us.ap, ALU.mult, [ac, us], [aT])
        P.barrier()
        for b in banks:
            b.w.clear()
            b.r.clear()
        A.reset(base)
        wd = [A.take("wd", [128, NFC, 256], BF16) for _ in range(2)]
        yst = [A.take("yst", [128, 256], F32) for _ in range(4)]
        ggb_ = A.take("GGb", [128, D], F32)
        yt = [A.take("yt", [128, D], F32) for _ in range(2)]
        xt2 = [A.take("xt2", [128, D], F32) for _ in range(2)]
        junk2 = A.take("junk2", [128, D], BF16)
        wd3 = w_down.ap[layer].rearrange("(c p) n -> p c n", p=128)

        def load_wd(nb):
            P.dma("pool", wd[nb % 2].ap, wd3[:, :, nb * 256:(nb + 1) * 256], [w_down], [wd[nb % 2]], wd[nb % 2])
        load_wd(0)
        load_wd(1)
        P.dma("sp", ggb_.ap, ngrows.ap[layer * 4 + 3:layer * 4 + 4, :].partition_broadcast(128), [ngrows], [ggb_], ggb_)
        P.dma("sp", yt[0].ap, grow_ap(layer, 1), [mall], [yt[0]], yt[0])
        P.tt("pool", ggb_.ap, ggb_.ap, yt[0].ap, ALU.mult, [ggb_, yt[0]], [ggb_])
        bi = 0
        for nb in range(8):
            wdb = wd[nb % 2]
            for t in range(NT):
                pp = banks[bi % 6]
                ys = yst[bi % 4]
                bi += 1
                for fc in range(NFC):
                    P.mm(pp[:, 0:256], aT[:, fc, t * 128:(t + 1) * 128], wdb[:, fc, :],
                         fc == 0, fc == NFC - 1, [aT, wdb], [pp], signal=(fc == NFC - 1))
                if bi % 2 == 0:
                    P.act(ys.ap, pp[:, 0:256], AF.Copy, [pp], [ys])
                else:
                    P.copy("dve", ys.ap, pp[:, 0:256], [pp], [ys])
                P.dma("sp", yscr.ap[t * 128:(t + 1) * 128, nb * 256:(nb + 1) * 256], ys.ap, [ys], [yscr], ys)
            if nb + 2 < 8:
                load_wd(nb + 2)
        emit_resid(P, yscr, xmid, xdst, ggb_, yt, xt2, stat, junk2)

    xcur = x_in
    for layer in range(nlayers):
        kind = "ab" if layer % 2 == 0 else "c"
        stage_LA(kind, layer, xcur)
        stage_LB(kind, layer, xcur)
        if debug_out:
            dm = P.dram("dbg_m%d" % layer, [T, D], F32, "ExternalOutput")
            P.dma("sp", dm.ap, xmid.ap, [xmid], [dm], stat[0])
        xdst = out if layer == nlayers - 1 else xnext[layer % 2]
        stage_LC(layer, xdst)
        if debug_out and layer != nlayers - 1:
            dx = P.dram("dbg_x%d" % layer, [T, D], F32, "ExternalOutput")
            P.dma("sp", dx.ap, xdst.ap, [xdst], [dx], stat[0])
        xcur = xdst
    return P.finish()
import math
from concourse.bass_utils import run_bass_kernel_spmd
_BF = ml_dtypes.bfloat16
_NC = {}


def _fused_inputs(x, c, ada_w, ada_b, norm_g, ab_w_in, ab_w_out, a_sink, c_w_in, c_w_out,
                  c_lambda, c_subln_g, ffn_w_up, ffn_conv_w, ffn_conv_b, ffn_w_down):
    xf = x.reshape(8192, 2048)
    ident = np.eye(128, dtype=_BF)
    identf = np.eye(128, dtype=np.float32)
    ngT = np.ascontiguousarray(norm_g.reshape(4, 4, 16, 128).transpose(3, 0, 1, 2))
    ngrows = np.ascontiguousarray(norm_g.reshape(16, 2048))
    sinkb = np.ascontiguousarray(np.broadcast_to(a_sink[None], (128, 2, 4)))
    lamb = np.ascontiguousarray(np.broadcast_to(c_lambda[None], (128, 2, 4, 128)))
    sublnb = np.ascontiguousarray(np.broadcast_to(c_subln_g[None], (128, 2, 256)))
    lis = [0.8 - 0.6 * math.exp(-0.3 * l) for l in (1, 3)]
    lin = np.ascontiguousarray(np.broadcast_to(np.array([[li, 1.0 - li] for li in lis], np.float32)[None], (128, 2, 2)))
    cw = np.concatenate([ffn_conv_w, ffn_conv_b[:, None, :]], 1)
    convT = np.ascontiguousarray(cw.reshape(4, 4, 43, 128).transpose(3, 0, 1, 2))
    lt = {tp: ltile_host(tp) for tp in TYPES}
    adaq = [np.ascontiguousarray(ada_w[:, :, q * 3072:(q + 1) * 3072]) for q in range(4)]
    maps = []
    p = np.arange(128)
    for core in range(8):
        b, qr = core // 4, core % 4
        idx = np.full((128, 43), BIGIDX, np.int32)
        for s in range(3):
            r = qr - 1 + s
            if 0 <= r <= 3:
                for hd in range(6):
                    idx[:, s * 6 + hd] = (hd // 3) * 1536 + r * 384 + (hd % 3) * 128 + p
        for wk in range(24):
            tok = qr * 1024 - 1024 + wk * 128 + p
            ok = (tok >= 0) & (tok < 4096)
            tl = tok % 1024
            row = (tl // 512) * 2048 + (tok // 1024) * 512 + tl % 512
            idx[:, 18 + wk] = np.where(ok, row, BIGIDX)
        if qr > 0:
            idx[0, 42] = (qr - 1) * 2 + 1
        if qr < 3:
            idx[1, 42] = (qr + 1) * 2
        fl = np.zeros((128, 2), np.float32)
        fl[:, 0] = 1.0 if qr > 0 else 0.0
        fl[:, 1] = 1.0 if qr < 3 else 0.0
        m = {"x": np.ascontiguousarray(xf[core * 1024:(core + 1) * 1024]),
             "cT": np.ascontiguousarray(c[b].reshape(16, 128).T.reshape(128, 16, 1)),
             "ada_wq": adaq[qr], "ada_bq": np.ascontiguousarray(ada_b[:, qr * 3072:(qr + 1) * 3072].reshape(1, 12288)),
             "ngT": ngT, "ngrows": ngrows, "ab_w_in": ab_w_in, "ab_w_out": ab_w_out, "sinkb": sinkb,
             "c_w_in": c_w_in, "c_w_out": c_w_out, "lamb": lamb, "sublnb": sublnb, "lin": lin,
             "w_up": ffn_w_up, "w_down": ffn_w_down, "convT": convT, "Lc": lc_host(qr),
             "ident": ident, "identf": identf, "flags": fl, "idxkv": idx}
        for tp in TYPES:
            m["L" + tp] = lt[tp]
        maps.append(m)
    return maps


def kernel(x, c, ada_w, ada_b, norm_g, ab_w_in, ab_w_out, a_sink, c_w_in, c_w_out,
           c_lambda, c_subln_g, ffn_w_up, ffn_conv_w, ffn_conv_b, ffn_w_down, _nlayers=4, _debug=None):
    f = lambda a: np.ascontiguousarray(np.asarray(a, dtype=np.float32))
    args = [f(a) for a in (x, c, ada_w, ada_b, norm_g, ab_w_in, ab_w_out, a_sink, c_w_in, c_w_out,
                           c_lambda, c_subln_g, ffn_w_up, ffn_conv_w, ffn_conv_b, ffn_w_down)]
    maps = _fused_inputs(*args)
    key = (_nlayers, _debug is not None)
    if key not in _NC:
        _NC[key] = build_fused(_nlayers, debug_out=_debug is not None)
    res = run_bass_kernel_spmd(_NC[key], maps, core_ids=list(range(8)))
    if _debug is not None:
        _debug.extend(res.results)
    return np.concatenate([r["out"] for r in res.results], 0).reshape(2, 4096, 2048).astype(np.float32)
```

```python
from contextlib import ExitStack
import numpy as np
import concourse.bass as bass
import concourse.mybir as mybir

F32 = mybir.dt.float32
BF16 = mybir.dt.bfloat16
AF = mybir.ActivationFunctionType
ALU = mybir.AluOpType
AX = mybir.AxisListType

ENGS = ("pe", "act", "dve", "pool", "sp")


class Buf:
    __slots__ = ("ap", "w", "r", "semkey", "name", "t", "persistent")

    def __init__(self, name, t, ap):
        self.name = name
        self.t = t
        self.ap = ap
        self.w = {}
        self.r = {}
        self.semkey = None
        self.persistent = False

    def __getitem__(self, idx):
        return self.ap[idx]


class Prog:
    def __init__(self):
        self.nc = bass.Bass("TRN2", target_bir_lowering=False)
        self.stack = ExitStack()
        self.sems = {}
        self.semcnt = {}
        self.ops = {e: [] for e in ENGS}
        self.seen = {e: {} for e in ENGS}
        self.pending = {e: False for e in ENGS}
        self.outs = []
        self.nbuf = 0
        self.free_keys = []
        self.stage_bufs = []
        for e in ENGS[:4]:
            self._newsem(e)

    def _newsem(self, key):
        self.sems[key] = self.stack.enter_context(self.nc.semaphore("s_" + key))
        self.semcnt[key] = 0

    def sb(self, name, shape, dtype):
        t = self.stack.enter_context(self.nc.sbuf_tensor(name, list(shape), dtype))
        return Buf(name, t, t[:])

    def ps(self, name, shape, dtype):
        t = self.stack.enter_context(self.nc.psum_tensor(name, list(shape), dtype))
        return Buf(name, t, t[:])

    def dram(self, name, shape, dtype, kind="Internal"):
        t = self.nc.dram_tensor(name, list(shape), dtype, kind=kind)
        b = Buf(name, t, t.ap())
        if kind == "ExternalOutput":
            self.outs.append(b)
        return b

    def view(self, buf, ap):
        return ap

    def _need(self, eng, waits, key, val):
        if eng == "pe" and key == "pe":
            return
        if self.seen[eng].get(key, 0) >= val:
            return
        if waits.get(key, 0) < val:
            waits[key] = val

    def _deps(self, eng, reads, writes):
        waits = {}
        for b in reads:
            for k, v in b.w.items():
                self._need(eng, waits, k, v)
        for b in writes:
            for k, v in b.w.items():
                self._need(eng, waits, k, v)
            for k, v in b.r.items():
                self._need(eng, waits, k, v)
        for k, v in waits.items():
            self.seen[eng][k] = v
            self.ops[eng].append(("wait", k, v))

    def _mark(self, tok, reads, writes):
        k, v = tok
        for b in reads:
            if b.r.get(k, 0) < v:
                b.r[k] = v
        for b in writes:
            b.w[k] = v
            b.r.clear()

    def op(self, eng, fn, reads=(), writes=(), signal=True):
        self._deps(eng, reads, writes)
        if signal:
            self.semcnt[eng] += 1
            tok = (eng, self.semcnt[eng])
            self.pending[eng] = False
            self.ops[eng].append(("op", fn, eng, 1))
        else:
            tok = (eng, self.semcnt[eng] + 1)
            self.pending[eng] = True
            self.ops[eng].append(("op", fn, None, 0))
        self._mark(tok, reads, writes)

    def dma(self, q, out, in_, reads, writes, sembuf):
        key = self.semkey_for(sembuf)
        self._deps(q, reads, writes)
        self.semcnt[key] += 16
        tok = (key, self.semcnt[key])
        self.ops[q].append(("op", lambda e: e.dma_start(out=out, in_=in_), key, 16))
        self._mark(tok, reads, writes)

    def op_sem(self, eng, fn, reads, writes, key, inc):
        if key not in self.sems:
            self._newsem(key)
        self._deps(eng, reads, writes)
        self.semcnt[key] += inc
        self.ops[eng].append(("op", fn, key, inc))
        self._mark((key, self.semcnt[key]), reads, writes)

    def semkey_for(self, sembuf):
        if sembuf.semkey is None:
            if self.free_keys:
                sembuf.semkey = self.free_keys.pop()
            else:
                self.nbuf += 1
                sembuf.semkey = "d%d" % self.nbuf
                self._newsem(sembuf.semkey)
            self.stage_bufs.append(sembuf)
        return sembuf.semkey

    def release_stage_sems(self):
        for b in self.stage_bufs:
            if not getattr(b, "persistent", False):
                self.free_keys.append(b.semkey)
                b.semkey = None
        self.stage_bufs = [b for b in self.stage_bufs if b.semkey is not None]

    def barrier(self):
        for e in ENGS:
            assert not self.pending[e]
            waits = {}
            for k, v in self.semcnt.items():
                if v > 0 and k != "cc":
                    self._need(e, waits, k, v)
            for k, v in waits.items():
                self.seen[e][k] = v
                self.ops[e].append(("wait", k, v))

    def mm(self, out, lhsT, rhs, start, stop, R, W, signal=True):
        self.op("pe", lambda e: e.matmul(out, lhsT=lhsT, rhs=rhs, start=start, stop=stop), R, W, signal)

    def tr(self, out, in_, ident, R, W, signal=True):
        self.op("pe", lambda e: e.transpose(out, in_, ident), R, W, signal)

    def act(self, out, in_, func, R, W, bias=None, scale=None, accum_out=None, eng="act"):
        kw = {}
        if bias is not None:
            kw["bias"] = bias
        if scale is not None:
            kw["scale"] = scale
        if accum_out is not None:
            kw["accum_out"] = accum_out
        self.op(eng, lambda e: e.activation(out=out, in_=in_, func=func, **kw), R, W)

    def ts(self, eng, out, in0, s1, s2, op0, op1, R, W, accum_out=None):
        if op1 is None:
            self.op(eng, lambda e: e.tensor_scalar(out=out, in0=in0, scalar1=s1, scalar2=None, op0=op0), R, W)
        else:
            self.op(eng, lambda e: e.tensor_scalar(out=out, in0=in0, scalar1=s1, scalar2=s2, op0=op0, op1=op1), R, W)

    def stt(self, eng, out, in0, scalar, in1, op0, op1, R, W):
        self.op(eng, lambda e: e.scalar_tensor_tensor(out=out, in0=in0, scalar=scalar, in1=in1, op0=op0, op1=op1), R, W)

    def tt(self, eng, out, in0, in1, op, R, W):
        self.op(eng, lambda e: e.tensor_tensor(out=out, in0=in0, in1=in1, op=op), R, W)

    def copy(self, eng, out, in_, R, W):
        if eng == "act":
            self.op(eng, lambda e: e.copy(out=out, in_=in_), R, W)
        else:
            self.op(eng, lambda e: e.tensor_copy(out=out, in_=in_), R, W)

    def memset(self, eng, ap, val, W):
        self.op(eng, lambda e: e.memset(ap, val), (), W)

    def finish(self):
        waits = {}
        for b in self.outs:
            for k, v in b.w.items():
                self._need("sp", waits, k, v)
        for k, v in waits.items():
            self.seen["sp"][k] = v
            self.ops["sp"].append(("wait", k, v))
        for e in ENGS:
            assert not self.pending[e], e
        nc = self.nc
        sems = self.sems

        def replay(lst):
            def body(e):
                for it in lst:
                    if it[0] == "wait":
                        e.wait_ge(sems[it[1]], it[2])
                    else:
                        ins = it[1](e)
                        if it[3]:
                            ins.then_inc(sems[it[2]], it[3])
            return body

        with nc.Block() as block:
            block.tensor(replay(self.ops["pe"]))
            block.scalar(replay(self.ops["act"]))
            block.vector(replay(self.ops["dve"]))
            block.gpsimd(replay(self.ops["pool"]))
            block.sync(replay(self.ops["sp"]))
        self.stack.close()
        return nc
import ml_dtypes

D = 2048
DC = 16
T = 1024
NT = 8
EPS = 1e-6
HD = 128
SCALE = HD ** -0.5


def emit_mod_prep(P, modT, ngT, gi, si, ni, GT, SHT):
    P.stt("dve", GT.ap, modT[:, si, :], 1.0, ngT[:, ni, :], ALU.add, ALU.mult, [modT, ngT], [GT])
    P.copy("dve", SHT.ap, modT[:, gi, :], [modT], [SHT])


def emit_norm_T(P, x_dram, row0, nrows, hT, col0, GT, SHT, ident, xb, xnb, junk, stat, pst, ti):
    xt = xb[ti % 2]
    xn = xnb[ti % 2]
    if x_dram is not None:
        P.dma("sp", xt[0:nrows, :], x_dram[row0:row0 + nrows, :], [x_dram], [xt], xt)
    st = stat[ti % len(stat)]
    ss = st[:, 0:1]
    rs = st[:, 1:2]
    P.act(junk.ap, xt.ap, AF.Square, [xt], [junk, st], accum_out=ss)
    P.ts("dve", rs, ss, 1.0 / D, EPS, ALU.mult, ALU.add, [st], [st])
    P.act(rs, rs, AF.Sqrt, [st], [st])
    P.op("dve", lambda e: e.reciprocal(out=rs, in_=rs), [st], [st])
    P.ts("dve", xn.ap, xt.ap, rs, None, ALU.mult, None, [xt, st], [xn])
    for g in range(2):
        pt = pst[g]
        for i in range(8):
            c = g * 8 + i
            P.tr(pt[:, i, :], xn[:, c * 128:(c + 1) * 128], ident.ap, [xn, ident], [pt], signal=(i == 7))
        for i in range(8):
            c = g * 8 + i
            dst = hT[:, c, col0:col0 + nrows]
            if i % 2 == 0:
                P.act(dst, pt[:, i, 0:nrows], AF.Identity, [pt, GT, SHT], [hT],
                      bias=SHT[:, c:c + 1], scale=GT[:, c:c + 1])
            else:
                P.ts("dve", dst, pt[:, i, 0:nrows], GT[:, c:c + 1], SHT[:, c:c + 1], ALU.mult, ALU.add,
                     [pt, GT, SHT], [hT])


def build_LA(kind):
    P = Prog()
    W = 3584 if kind == "ab" else 6144
    NQ = 16
    NK = 6 if kind == "ab" else 16
    NV = 768 if kind == "ab" else 2048
    x = P.dram("x", [T, D], F32, "ExternalInput")
    modT = P.dram("modT", [128, 6, 16], F32, "ExternalInput")
    ngT = P.dram("ngT", [128, 4, 16], F32, "ExternalInput")
    w_in = P.dram("w_in", [D, W], F32, "ExternalInput")
    identd = P.dram("ident", [128, 128], BF16, "ExternalInput")
    qT = P.dram("qT", [NQ, 128, T], BF16, "ExternalOutput")
    kT = P.dram("kT", [NK, 128, T], BF16, "ExternalOutput")
    v = P.dram("v", [T, NV], BF16, "ExternalOutput")

    ident = P.sb("ident_s", [128, 128], BF16)
    modS = P.sb("modS", [128, 6, 16], F32)
    ngS = P.sb("ngS", [128, 4, 16], F32)
    GT = P.sb("GT", [128, 16], F32)
    SHT = P.sb("SHT", [128, 16], F32)
    hT = P.sb("hT", [128, DC, T], BF16)
    xb = [P.sb("xb%d" % i, [128, D], F32) for i in range(2)]
    xnb = [P.sb("xnb%d" % i, [128, D], BF16) for i in range(2)]
    junk = P.sb("junk", [128, D], BF16)
    stat = P.sb("stat", [128, 32], F32)
    pst = [P.ps("pst%d" % i, [128, 8, 128], BF16) for i in range(2)]
    psm = [P.ps("psm%d" % i, [128, 512], F32) for i in range(4)]
    wb = [P.sb("wb%d" % i, [128, DC, 512], BF16) for i in range(3)]
    stg_f = [P.sb("stgf%d" % i, [128, T], BF16) for i in range(2)]
    stg_t = [P.sb("stgt%d" % i, [128, 512], BF16) for i in range(3)]

    P.dma("sp", ident.ap, identd.ap, [identd], [ident], ident)
    P.dma("sp", modS.ap, modT.ap, [modT], [modS], modS)
    P.dma("sp", ngS.ap, ngT.ap, [ngT], [ngS], ngS)
    P.memset("dve", stat.ap, 0.0, [stat])
    emit_mod_prep(P, modS, ngS, 0, 1, 0, GT, SHT)

    if kind == "ab":
        blocks = [
            (0, [(0, ("q", 0)), (128, ("q", 1)), (256, ("q", 2)), (384, ("q", 3))], []),
            (512, [(0, ("k", 0)), (128, ("k", 1))], [(256, 256, 0)]),
            (1024, [(i * 128, ("q", 4 + i)) for i in range(4)], []),
            (1536, [(i * 128, ("q", 8 + i)) for i in range(4)], []),
            (2048, [(i * 128, ("q", 12 + i)) for i in range(4)], []),
            (2560, [(i * 128, ("k", 2 + i)) for i in range(4)], []),
            (3072, [], [(0, 512, 256)]),
        ]
    else:
        blocks = []
        for i in range(4):
            blocks.append((i * 512, [(j * 128, ("q", i * 4 + j)) for j in range(4)], []))
        for i in range(4):
            blocks.append((2048 + i * 512, [(j * 128, ("k", i * 4 + j)) for j in range(4)], []))
        for i in range(4):
            blocks.append((4096 + i * 512, [], [(0, 512, i * 512)]))
    w3 = w_in.ap.rearrange("(c p) n -> p c n", p=128)

    def load_w(bi):
        c0 = blocks[bi][0]
        b = wb[bi % 3]
        P.dma("pool", b.ap, w3[:, :, c0:c0 + 512], [w_in], [b], b)

    load_w(0)
    load_w(1)
    for t in range(NT):
        emit_norm_T(P, x, t * 128, 128, hT, t * 128, GT, SHT, ident, xb, xnb, junk, stat, pst, t)

    pi = 0
    fi = 0
    ti = 0
    for bi, (c0, funits, tranges) in enumerate(blocks):
        if bi + 2 < len(blocks):
            load_w(bi + 2)
        b = wb[bi % 3]
        for (off, (which, idx)) in funits:
            sg = stg_f[fi % 2]
            fi += 1
            for half in range(2):
                pp = psm[pi % 4]
                pi += 1
                for c in range(DC):
                    P.mm(pp.ap, b[:, c, off:off + 128], hT[:, c, half * 512:(half + 1) * 512],
                         c == 0, c == DC - 1, [b, hT], [pp], signal=(c == DC - 1))
                dst = sg[:, half * 512:(half + 1) * 512]
                sc = SCALE if which == "q" else 1.0
                if half == 0:
                    P.act(dst, pp.ap, AF.Copy, [pp], [sg], scale=sc)
                else:
                    P.ts("dve", dst, pp.ap, sc, None, ALU.mult, None, [pp], [sg])
            dd = qT if which == "q" else kT
            P.dma("sp", dd.ap[idx], sg.ap, [sg], [dd], sg)
        for (off, wdt, dcol) in tranges:
            for t in range(NT):
                pp = psm[pi % 4]
                pi += 1
                for c in range(DC):
                    P.mm(pp[:, 0:wdt], hT[:, c, t * 128:(t + 1) * 128], b[:, c, off:off + wdt],
                         c == 0, c == DC - 1, [b, hT], [pp], signal=(c == DC - 1))
                sg = stg_t[ti % 3]
                ti += 1
                if t % 2 == 0:
                    P.act(sg[:, 0:wdt], pp[:, 0:wdt], AF.Copy, [pp], [sg])
                else:
                    P.copy("dve", sg[:, 0:wdt], pp[:, 0:wdt], [pp], [sg])
                P.dma("sp", v.ap[t * 128:(t + 1) * 128, dcol:dcol + wdt], sg[:, 0:wdt], [sg], [v], sg)
    return P.finish()


class Arena:
    def __init__(self, P, name, words):
        self.P = P
        self.buf = P.sb(name, [128, words], F32)
        self.words = words
        self.off = 0
        self.n = 0

    def reset(self, off=0):
        self.off = off

    def take(self, name, shape, dtype):
        n = 1
        for s in shape[1:]:
            n *= s
        words = n if dtype == F32 else (n + 1) // 2
        a = self.off
        self.off += words
        assert self.off <= self.words, (name, self.off, self.words)
        ap = self.buf.ap[:, a:a + words]
        if dtype != F32:
            ap = ap.bitcast(dtype)
            if n % 2:
                ap = ap[:, 0:n]
        if len(shape) == 3:
            ap = ap.rearrange("p (a b) -> p a b", b=shape[2])
        self.n += 1
        return Buf("%s_%d" % (name, self.n), self.buf.t, ap)


DFF = 5504
NFC = 43


def build_LC():
    P = Prog()
    xm = P.dram("xm", [T, D], F32, "ExternalInput")
    halo = P.dram("halo", [2, D], F32, "ExternalInput")
    flags = P.dram("flags", [128, 2], F32, "ExternalInput")
    modT = P.dram("modT", [128, 6, 16], F32, "ExternalInput")
    ngT = P.dram("ngT", [128, 4, 16], F32, "ExternalInput")
    rows = P.dram("rows", [2, D], F32, "ExternalInput")
    w_up = P.dram("w_up", [D, 2 * DFF], F32, "ExternalInput")
    convT = P.dram("convT", [128, 4, NFC], F32, "ExternalInput")
    w_down = P.dram("w_down", [DFF, D], F32, "ExternalInput")
    identd = P.dram("ident", [128, 128], BF16, "ExternalInput")
    xo = P.dram("xo", [T, D], F32, "ExternalOutput")
    yscr = P.dram("yscr", [T, D], F32)

    ident = P.sb("ident_s", [128, 128], BF16)
    modS = P.sb("modS", [128, 6, 16], F32)
    ngS = P.sb("ngS", [128, 4, 16], F32)
    GT = P.sb("GT", [128, 16], F32)
    SHT = P.sb("SHT", [128, 16], F32)
    convS = P.sb("convS", [128, 4, NFC], F32)
    flagS = P.sb("flagS", [128, 2], F32)
    stat = P.sb("stat", [128, 32], F32)
    aT = P.sb("aT", [128, NFC, T], BF16)
    banks = [P.ps("bank%d" % i, [128, 512], F32) for i in range(8)]
    A = Arena(P, "arena", 24064)

    P.dma("sp", ident.ap, identd.ap, [identd], [ident], ident)
    P.dma("sp", modS.ap, modT.ap, [modT], [modS], modS)
    P.dma("sp", ngS.ap, ngT.ap, [ngT], [ngS], ngS)
    P.dma("sp", convS.ap, convT.ap, [convT], [convS], convS)
    P.dma("sp", flagS.ap, flags.ap, [flags], [flagS], flagS)
    P.memset("dve", stat.ap, 0.0, [stat])
    emit_mod_prep(P, modS, ngS, 3, 4, 2, GT, SHT)

    h2T = A.take("h2T", [128, DC, T + 2], BF16)
    wg = [A.take("wg", [128, DC, 256], BF16) for _ in range(2)]
    wu = [A.take("wu", [128, DC, 256], BF16) for _ in range(2)]
    mark = A.off
    xb = [A.take("xb", [128, D], F32) for _ in range(2)]
    xnb = [A.take("xnb", [128, D], BF16) for _ in range(2)]
    junk = A.take("junk", [128, D], BF16)
    w3 = w_up.ap.rearrange("(c p) n -> p c n", p=128)
    NG = (NFC + 1) // 2

    def load_wu(g):
        c0 = g * 256
        wd_ = min(256, DFF - c0)
        P.dma("pool", wg[g % 2][:, :, 0:wd_], w3[:, :, c0:c0 + wd_], [w_up], [wg[g % 2]], wg[g % 2])
        P.dma("pool", wu[g % 2][:, :, 0:wd_], w3[:, :, DFF + c0:DFF + c0 + wd_], [w_up], [wu[g % 2]], wu[g % 2])

    load_wu(0)
    load_wu(1)
    pst = []
    for i in (6, 7):
        b = banks[i]
        pst.append(Buf("pst%d" % i, b.t, b.ap.bitcast(BF16).rearrange("p (a b) -> p a b", b=128)))
        pst[-1].w = b.w
        pst[-1].r = b.r
    for t in range(NT):
        emit_norm_T(P, xm, t * 128, 128, h2T, t * 128, GT, SHT, ident, xb, xnb, junk, stat, pst, t)
    emit_norm_T(P, halo, 0, 2, h2T, T, GT, SHT, ident, xb, xnb, junk, stat, pst, NT)
    P.barrier()
    for b in banks:
        b.w.clear()
        b.r.clear()

    A.reset(mark)
    gsb = [A.take("gsb", [128, T + 4], F32) for _ in range(2)]
    acc = [A.take("acc", [128, T], F32) for _ in range(2)]
    gg = acc
    usb = [A.take("usb", [128, T], F32) for _ in range(2)]
    sets = [(banks[0], banks[1]), (banks[2], banks[3]), (banks[4], banks[5])]
    si = 0
    ph = banks[6]
    for fc in range(NFC):
        g = fc // 2
        off = (fc % 2) * 128
        if fc % 2 == 0 and g >= 1 and g + 1 < NG:
            load_wu(g + 1)
        wgb, wub = wg[g % 2], wu[g % 2]
        gs, ac, ggb, us = gsb[fc % 2], acc[fc % 2], gg[fc % 2], usb[fc % 2]
        hc = (fc % 8) * 2
        sg = sets[si % 3]
        si += 1
        for half in range(2):
            for c in range(DC):
                P.mm(sg[half].ap, wgb[:, c, off:off + 128], h2T[:, c, half * 512:(half + 1) * 512],
                     c == 0, c == DC - 1, [wgb, h2T], [sg[half]], signal=(c == DC - 1))
        for c in range(DC):
            P.mm(ph[:, hc:hc + 2], wgb[:, c, off:off + 128], h2T[:, c, T:T + 2],
                 c == 0, c == DC - 1, [wgb, h2T], [ph], signal=(c == DC - 1))
        P.act(gs[:, 1:513], sg[0].ap, AF.Copy, [sg[0]], [gs])
        P.act(gs[:, 513:1025], sg[1].ap, AF.Copy, [sg[1]], [gs])
        P.tt("dve", gs[:, 0:1], ph[:, hc:hc + 1], flagS[:, 0:1], ALU.mult, [ph, flagS], [gs])
        P.tt("dve", gs[:, 1025:1026], ph[:, hc + 1:hc + 2], flagS[:, 1:2], ALU.mult, [ph, flagS], [gs])
        su = sets[si % 3]
        si += 1
        for half in range(2):
            for c in range(DC):
                P.mm(su[half].ap, wub[:, c, off:off + 128], h2T[:, c, half * 512:(half + 1) * 512],
                     c == 0, c == DC - 1, [wub, h2T], [su[half]], signal=(c == DC - 1))
        P.act(us[:, 0:512], su[0].ap, AF.Copy, [su[0]], [us])
        P.act(us[:, 512:1024], su[1].ap, AF.Copy, [su[1]], [us])
        P.ts("dve", ac.ap, gs[:, 0:T], convS[:, 0, fc:fc + 1], None, ALU.mult, None, [gs, convS], [ac])
        P.stt("dve", ac.ap, gs[:, 1:T + 1], convS[:, 1, fc:fc + 1], ac.ap, ALU.mult, ALU.add, [gs, convS, ac], [ac])
        P.stt("dve", ac.ap, gs[:, 2:T + 2], convS[:, 2, fc:fc + 1], ac.ap, ALU.mult, ALU.add, [gs, convS, ac], [ac])
        P.act(ggb.ap, ac.ap, AF.Gelu_apprx_tanh, [ac, convS], [ggb], bias=convS[:, 3, fc:fc + 1])
        P.tt("pool", aT[:, fc, :], ggb.ap, us.ap, ALU.mult, [ggb, us], [aT])
    P.barrier()

    A.reset(0)
    wd = [A.take("wd", [128, NFC, 256], BF16) for _ in range(2)]
    yst = [A.take("yst", [128, 256], F32) for _ in range(4)]
    ggb_ = A.take("GGb", [128, D], F32)
    yt = [A.take("yt", [128, D], F32) for _ in range(2)]
    xt2 = [A.take("xt2", [128, D], F32) for _ in range(2)]
    junk2 = A.take("junk2", [128, D], BF16)
    grow = yt[0]
    wd3 = w_down.ap.rearrange("(c p) n -> p c n", p=128)

    def load_wd(nb):
        P.dma("pool", wd[nb % 2].ap, wd3[:, :, nb * 256:(nb + 1) * 256], [w_down], [wd[nb % 2]], wd[nb % 2])

    load_wd(0)
    load_wd(1)
    P.dma("sp", ggb_.ap, rows.ap[0:1, :].partition_broadcast(128), [rows], [ggb_], ggb_)
    P.dma("sp", grow.ap, rows.ap[1:2, :].partition_broadcast(128), [rows], [grow], grow)
    P.tt("pool", ggb_.ap, ggb_.ap, grow.ap, ALU.mult, [ggb_, grow], [ggb_])
    bi = 0
    for nb in range(8):
        wdb = wd[nb % 2]
        for t in range(NT):
            pp = banks[bi % 6]
            ys = yst[bi % 4]
            bi += 1
            for fc in range(NFC):
                P.mm(pp[:, 0:256], aT[:, fc, t * 128:(t + 1) * 128], wdb[:, fc, :],
                     fc == 0, fc == NFC - 1, [aT, wdb], [pp], signal=(fc == NFC - 1))
            if bi % 2 == 0:
                P.act(ys.ap, pp[:, 0:256], AF.Copy, [pp], [ys])
            else:
                P.copy("dve", ys.ap, pp[:, 0:256], [pp], [ys])
            P.dma("sp", yscr.ap[t * 128:(t + 1) * 128, nb * 256:(nb + 1) * 256], ys.ap, [ys], [yscr], ys)
        if nb + 2 < 8:
            load_wd(nb + 2)

    emit_resid(P, yscr, xm, xo, ggb_, yt, xt2, stat, junk2)
    return P.finish()


def emit_resid(P, ysrc, xsrc, xdst, GGb, yt, xt2, stat, junk):
    for t in range(NT):
        y = yt[t % 2]
        xx = xt2[t % 2]
        P.dma("sp", y.ap, ysrc.ap[t * 128:(t + 1) * 128, :], [ysrc], [y], y)
        P.dma("sp", xx.ap, xsrc.ap[t * 128:(t + 1) * 128, :], [xsrc], [xx], xx)
        st = stat[t % len(stat)]
        ss = st[:, 2:3]
        rs = st[:, 3:4]
        P.act(junk.ap, y.ap, AF.Square, [y], [junk, st], accum_out=ss)
        P.ts("dve", rs, ss, 1.0 / D, EPS, ALU.mult, ALU.add, [st], [st])
        P.act(rs, rs, AF.Sqrt, [st], [st])
        P.op("dve", lambda e, rs=rs: e.reciprocal(out=rs, in_=rs), [st], [st])
        P.stt("dve", y.ap, y.ap, rs, GGb.ap, ALU.mult, ALU.mult, [y, st, GGb], [y])
        P.tt("pool", xx.ap, xx.ap, y.ap, ALU.add, [xx, y], [xx])
        P.dma("pool", xdst.ap[t * 128:(t + 1) * 128, :], xx.ap, [xx], [xdst], xx)


SLOPES16 = [2.0 ** (-8.0 * (i + 1) / 16) for i in range(16)]
SLOPES8 = [2.0 ** (-8.0 * (i + 1) / 8) for i in range(8)]
TYPES = {"A": (128, 1, 128, -512), "g0": (64, 1, 128, -512), "g1": (256, 4, 256, -640), "g2": (1024, 16, 1024, -1408)}
BIG = 1e30


def ltile_host(tp):
    R_, dil, dmax, dmin = TYPES[tp]
    wid = dmax - dmin + 512
    i = np.arange(128)[:, None]
    c = np.arange(wid)[None, :]
    d = c + dmin - i
    ok = (np.abs(d) <= R_) & (d % dil == 0)
    return np.where(ok, np.abs(d), BIG).astype(np.float32)


def lc_host(qr):
    i = np.arange(128)[:, None]
    c = np.arange(4992)[None, :]
    return np.abs(c - 3968 + qr * 1024 - i).astype(np.float32)


def build_LB(kind):
    P = Prog()
    ab = kind == "ab"
    KC = 8 if ab else 16
    VW = 129 if ab else 257
    qT = P.dram("qT", [16, 128, T], BF16, "ExternalInput")
    if ab:
        kT = P.dram("kT", [6, 128, 3072], BF16, "ExternalInput")
        vv = P.dram("vv", [3072, 6, 129], BF16, "ExternalInput")
        Ld = {tp: P.dram("L" + tp, [128, TYPES[tp][2] - TYPES[tp][3] + 512], F32, "ExternalInput") for tp in TYPES}
        sinkd = P.dram("sinkb", [128, 4], F32, "ExternalInput")
    else:
        kT = P.dram("kT", [16, 128, 4096], BF16, "ExternalInput")
        vv = P.dram("vv", [4096, 2048], BF16, "ExternalInput")
        Lcd = P.dram("Lc", [128, 4992], F32, "ExternalInput")
        lambd = P.dram("lamb", [128, 4, 128], F32, "ExternalInput")
        sgd = P.dram("sublnb", [128, 256], F32, "ExternalInput")
        lind = P.dram("lin", [128, 2], F32, "ExternalInput")
    x = P.dram("x", [T, D], F32, "ExternalInput")
    rows = P.dram("rows", [2, D], F32, "ExternalInput")
    w_out = P.dram("w_out", [KC * 128, D], F32, "ExternalInput")
    identd = P.dram("ident", [128, 128], BF16, "ExternalInput")
    xo = P.dram("xo", [T, D], F32, "ExternalOutput")
    yscr = P.dram("yscr", [T, D], F32)

    ident = P.sb("ident_s", [128, 128], BF16)
    stat = P.sb("stat", [128, 32], F32)
    oT = P.sb("oT_all", [128, KC, T], BF16)
    tmp = [P.sb("tmp%d" % i, [128, 512], F32) for i in range(2)]
    PT = [P.sb("PT%d" % i, [128, 512], BF16) for i in range(2)]
    sm = P.sb("small", [128, 64], F32)
    on = [P.sb("on%d" % i, [128, 4, VW - 1], BF16) for i in range(2)]
    banks = [P.ps("bank%d" % i, [128, 512], F32) for i in range(8)]
    sbank = banks[0:2]
    obank = banks[2:6]
    tb = banks[6]
    tbv = Buf("tbv", tb.t, tb.ap.bitcast(BF16).rearrange("p (a b) -> p a b", b=128))
    tbv.w = tb.w
    tbv.r = tb.r
    P.dma("sp", ident.ap, identd.ap, [identd], [ident], ident)
    P.memset("dve", stat.ap, 0.0, [stat])
    P.memset("dve", sm.ap, 0.0, [sm])

    def blocks(jobs, vw, first=True, last=True):
        n = len(jobs)
        cnt = blocks.cnt
        for i in range(n + 1):
            if i < n:
                kap, qap, vap, lap, slope, Rl = jobs[i]
                S = sbank[(cnt + i) % 2]
                tm = tmp[(cnt + i) % 2]
                pt = PT[(cnt + i) % 2]
                P.mm(S.ap, kap, qap, True, True, Rl, [S])
                P.stt("dve", tm.ap, lap, -slope, S.ap, ALU.mult, ALU.add, [S] + Rl, [tm])
                P.act(pt.ap, tm.ap, AF.Exp, [tm], [pt])
            if i >= 1:
                kap, qap, vap, lap, slope, Rl = jobs[i - 1]
                pt = PT[(cnt + i - 1) % 2]
                for qs in range(4):
                    P.mm(obank[qs][:, 0:vw], pt[:, qs * 128:(qs + 1) * 128], vap,
                         first and i == 1, last and i == n, [pt] + Rl, [obank[qs]],
                         signal=(qs == 3))
        blocks.cnt = cnt + n
    blocks.cnt = 0

    if ab:
        A = Arena(P, "arena", 26800)
        qa = A.take("qall", [128, 16, T], BF16)
        ka = A.take("kall", [128, 6, 3072], BF16)
        va = A.take("vall", [128, 24, 6 * 129], BF16)
        Ls = {tp: P.sb("Ls" + tp, [128, TYPES[tp][2] - TYPES[tp][3] + 512], F32) for tp in TYPES}
        P.dma("sp", qa.ap, qT.ap.rearrange("h p t -> p h t"), [qT], [qa], qa)
        P.dma("sp", ka.ap, kT.ap.rearrange("h p t -> p h t"), [kT], [ka], ka)
        P.dma("sp", va.ap, vv.ap.rearrange("(kt p) h n -> p kt (h n)", p=128), [vv], [va], va)
        for tp in TYPES:
            P.dma("sp", Ls[tp].ap, Ld[tp].ap, [Ld[tp]], [Ls[tp]], Ls[tp])
        P.dma("sp", sm[:, 0:4], sinkd.ap, [sinkd], [sm], sm)
        P.act(sm[:, 4:8], sm[:, 0:4], AF.Exp, [sm], [sm])
        outs = []
        for u in range(4):
            outs.append((u, [(u, "A", SLOPES16[u])], u // 2, u))
        for i in range(4):
            outs.append((4 + i, [(4 + i, "g0", SLOPES16[4 + i]), (8 + i, "g1", SLOPES16[8 + i]),
                                 (12 + i, "g2", SLOPES16[12 + i])], 2 + i, None))
        fin = 0
        for (ou, qlist, kv, sink) in outs:
            for qb in range(2):
                q0 = qb * 512
                jobs = []
                for (qu, tp, slope) in qlist:
                    R_, dil, dmax, dmin = TYPES[tp]
                    for dl in range(dmax, dmin - 1, -128):
                        wk = (q0 - dl + 1024) // 128
                        jobs.append((ka[:, kv, wk * 128:(wk + 1) * 128], qa[:, qu, q0:q0 + 512],
                                     va[:, wk, kv * 129:(kv + 1) * 129], Ls[tp][:, dl - dmin:dl - dmin + 512],
                                     slope, [ka, qa, va, Ls[tp]]))
                blocks(jobs, 129)
                onb = on[fin % 2]
                fin += 1
                for qs in range(4):
                    dn = sm[:, 8 + qs:9 + qs]
                    if sink is not None:
                        P.tt("dve", dn, obank[qs][:, 128:129], sm[:, 4 + sink:5 + sink], ALU.add, [obank[qs], sm], [sm])
                    else:
                        P.copy("dve", dn, obank[qs][:, 128:129], [obank[qs]], [sm])
                    P.op("dve", lambda e, dn=dn: e.reciprocal(out=dn, in_=dn), [sm], [sm])
                    P.ts("dve", onb[:, qs, :], obank[qs][:, 0:128], dn, None, ALU.mult, None, [obank[qs], sm], [onb])
                for qs in range(4):
                    P.tr(tbv[:, qs, :], onb[:, qs, :], ident.ap, [onb, ident], [tbv], signal=(qs == 3))
                P.act(oT[:, ou, q0:q0 + 512], tbv.ap[:, 0:4, :].rearrange("p a b -> p (a b)"), AF.Copy, [tbv], [oT])
    else:
        A = Arena(P, "arena", 20480)
        kh = [A.take("kh", [128, 2, 4096], BF16) for _ in range(2)]
        vh = [A.take("vh", [128, 32, 257], BF16) for _ in range(2)]
        qh = [A.take("qh", [128, 2, T], BF16) for _ in range(2)]
        Lc = P.sb("Lc_s", [128, 4992], F32)
        o1 = P.sb("o1", [128, 4, 257], F32)
        of = P.sb("of", [128, 256], F32)
        jk = P.sb("jk", [128, 256], BF16)
        lam = P.sb("lam_s", [128, 4, 128], F32)
        SG = P.sb("SG", [128, 256], F32)
        P.dma("sp", Lc.ap, Lcd.ap, [Lcd], [Lc], Lc)
        P.dma("sp", lam.ap, lambd.ap, [lambd], [lam], lam)
        P.dma("sp", SG.ap, sgd.ap, [sgd], [SG], SG)
        P.dma("sp", sm[:, 0:2], lind.ap, [lind], [sm], sm)
        for j in range(2):
            P.tt("dve", lam[:, 2 * j, :], lam[:, 2 * j, :], lam[:, 2 * j + 1, :], ALU.mult, [lam], [lam])
            P.op("dve", lambda e, j=j: e.tensor_reduce(out=sm[:, 4 + j:5 + j], in_=lam[:, 2 * j, :], axis=AX.X, op=ALU.add),
                 [lam], [sm])
        P.act(sm[:, 4:6], sm[:, 4:6], AF.Exp, [sm], [sm])
        P.tt("dve", sm[:, 2:3], sm[:, 4:5], sm[:, 5:6], ALU.subtract, [sm], [sm])
        P.tt("dve", sm[:, 2:3], sm[:, 2:3], sm[:, 0:1], ALU.add, [sm], [sm])
        P.ts("dve", sm[:, 3:4], sm[:, 2:3], -1.0, None, ALU.mult, None, [sm], [sm])
        P.ts("dve", SG.ap, SG.ap, sm[:, 1:2], None, ALU.mult, None, [SG, sm], [SG])
        vr = vv.ap.rearrange("(kt p) n -> p kt n", p=128)
        for i in range(2):
            P.memset("pool", vh[i][:, :, 256:257], 1.0, [vh[i]])

        def load_head(h):
            P.dma("sp", kh[h % 2].ap, kT.ap[2 * h:2 * h + 2].rearrange("j p t -> p j t"), [kT], [kh[h % 2]], kh[h % 2])
            P.dma("sp", vh[h % 2][:, :, 0:256], vr[:, :, h * 256:(h + 1) * 256], [vv], [vh[h % 2]], vh[h % 2])
            P.dma("sp", qh[h % 2].ap, qT.ap[2 * h:2 * h + 2].rearrange("j p t -> p j t"), [qT], [qh[h % 2]], qh[h % 2])

        load_head(0)
        fin = 0
        for h in range(8):
            if h + 1 < 8:
                load_head(h + 1)
            khb, vhb, qhb = kh[h % 2], vh[h % 2], qh[h % 2]
            for qb in range(2):
                q0 = qb * 512
                for j in range(2):
                    jobs = []
                    for kt in range(32):
                        c0 = q0 - kt * 128 + 3968
                        jobs.append((khb[:, j, kt * 128:(kt + 1) * 128], qhb[:, j, q0:q0 + 512], vhb[:, kt, :],
                                     Lc[:, c0:c0 + 512], SLOPES8[h], [khb, qhb, vhb, Lc]))
                    blocks(jobs, 257)
                    if j == 0:
                        for qs in range(4):
                            P.act(o1[:, qs, :], obank[qs][:, 0:257], AF.Copy, [obank[qs]], [o1])
                onb = on[fin % 2]
                fin += 1
                for qs in range(4):
                    r1 = sm[:, 8 + 4 * qs:9 + 4 * qs]
                    r2 = sm[:, 9 + 4 * qs:10 + 4 * qs]
                    ss = sm[:, 10 + 4 * qs:11 + 4 * qs]
                    rs = sm[:, 11 + 4 * qs:12 + 4 * qs]
                    P.op("dve", lambda e, r1=r1, qs=qs: e.reciprocal(out=r1, in_=o1[:, qs, 256:257]), [o1], [sm])
                    P.op("dve", lambda e, r2=r2, qs=qs: e.reciprocal(out=r2, in_=obank[qs][:, 256:257]), [obank[qs]], [sm])
                    P.tt("dve", r2, r2, sm[:, 3:4], ALU.mult, [sm], [sm])
                    P.ts("dve", of.ap, o1[:, qs, 0:256], r1, None, ALU.mult, None, [o1, sm], [of])
                    P.stt("dve", of.ap, obank[qs][:, 0:256], r2, of.ap, ALU.mult, ALU.add, [obank[qs], sm, of], [of])
                    P.memset("dve", ss, 0.0, [sm])
                    P.act(jk.ap, of.ap, AF.Square, [of], [jk, sm], accum_out=ss)
                    P.ts("dve", rs, ss, 1.0 / 256, EPS, ALU.mult, ALU.add, [sm], [sm])
                    P.act(rs, rs, AF.Sqrt, [sm], [sm])
                    P.op("dve", lambda e, rs=rs: e.reciprocal(out=rs, in_=rs), [sm], [sm])
                    P.stt("dve", onb[:, qs, :], of.ap, rs, SG.ap, ALU.mult, ALU.mult, [of, sm, SG], [onb])
                for qs in range(4):
                    for i in range(2):
                        P.tr(tbv[:, qs * 2 + i, :], onb[:, qs, i * 128:(i + 1) * 128], ident.ap, [onb, ident], [tbv],
                             signal=(qs == 3 and i == 1))
                tv = tbv.ap.rearrange("p (a i) b -> p i a b", i=2)
                for i in range(2):
                    P.act(oT[:, 2 * h + i, q0:q0 + 512].rearrange("p (a b) -> p a b", b=128), tv[:, i, :, :], AF.Copy, [tbv], [oT])

    P.barrier()
    A.reset(0)
    wo = [A.take("wo", [128, KC, 512], BF16) for _ in range(2)]
    yst = [A.take("yst", [128, 512], F32) for _ in range(2)]
    ggb_ = A.take("GGb", [128, D], F32)
    yt = [A.take("yt", [128, D], F32) for _ in range(2)]
    xt2 = [A.take("xt2", [128, D], F32) for _ in range(2)]
    junk2 = A.take("junk2", [128, D], BF16)
    wo3 = w_out.ap.rearrange("(c p) n -> p c n", p=128)

    def load_wo(nb):
        P.dma("pool", wo[nb % 2].ap, wo3[:, :, nb * 512:(nb + 1) * 512], [w_out], [wo[nb % 2]], wo[nb % 2])

    load_wo(0)
    load_wo(1)
    P.dma("sp", ggb_.ap, rows.ap[0:1, :].partition_broadcast(128), [rows], [ggb_], ggb_)
    P.dma("sp", yt[0].ap, rows.ap[1:2, :].partition_broadcast(128), [rows], [yt[0]], yt[0])
    P.tt("pool", ggb_.ap, ggb_.ap, yt[0].ap, ALU.mult, [ggb_, yt[0]], [ggb_])
    for b in banks:
        b.w.clear()
        b.r.clear()
    bi = 0
    for nb in range(4):
        wob = wo[nb % 2]
        for t in range(NT):
            pp = banks[bi % 8]
            ys = yst[bi % 2]
            bi += 1
            for c in range(KC):
                P.mm(pp.ap, oT[:, c, t * 128:(t + 1) * 128], wob[:, c, :], c == 0, c == KC - 1, [oT, wob], [pp],
                     signal=(c == KC - 1))
            if bi % 2 == 0:
                P.act(ys.ap, pp.ap, AF.Copy, [pp], [ys])
            else:
                P.copy("dve", ys.ap, pp.ap, [pp], [ys])
            P.dma("sp", yscr.ap[t * 128:(t + 1) * 128, nb * 512:(nb + 1) * 512], ys.ap, [ys], [yscr], ys)
        if nb + 2 < 4:
            load_wo(nb + 2)
    emit_resid(P, yscr, x, xo, ggb_, yt, xt2, stat, junk2)
    return P.finish()


def build_L0():
    P = Prog()
    cT = P.dram("cT", [128, 16, 2], F32, "ExternalInput")
    w = P.dram("w", [D, 6144], F32, "ExternalInput")
    bias = P.dram("bias", [2, 6144], F32, "ExternalInput")
    mod = P.dram("mod", [2, 6144], F32, "ExternalOutput")
    cS = P.sb("cS", [128, 16, 2], F32)
    bS = P.sb("bS", [2, 6144], F32)
    oS = P.sb("oS", [2, 6144], F32)
    wb = [P.sb("wb%d" % i, [128, 16, 512], F32) for i in range(2)]
    banks = [P.ps("bank%d" % i, [128, 512], F32) for i in range(2)]
    P.dma("sp", cS.ap, cT.ap, [cT], [cS], cS)
    P.dma("sp", bS.ap, bias.ap, [bias], [bS], bS)
    P.act(cS.ap, cS.ap, AF.Silu, [cS], [cS])
    w3 = w.ap.rearrange("(c p) n -> p c n", p=128)

    def load(nb):
        P.dma("sp", wb[nb % 2].ap, w3[:, :, nb * 512:(nb + 1) * 512], [w], [wb[nb % 2]], wb[nb % 2])

    load(0)
    load(1)
    for nb in range(12):
        pp = banks[nb % 2]
        for c in range(16):
            P.mm(pp[0:2, :], cS[:, c, :], wb[nb % 2][:, c, :], c == 0, c == 15, [cS, wb[nb % 2]], [pp], signal=(c == 15))
        P.tt("dve", oS[:, nb * 512:(nb + 1) * 512], pp[0:2, :], bS[:, nb * 512:(nb + 1) * 512], ALU.add, [pp, bS], [oS])
        if nb + 2 < 12:
            load(nb + 2)
    P.dma("sp", mod.ap, oS.ap, [oS], [mod], oS)
    return P.finish()
import math

BIGIDX = 1 << 24
ARENA_WORDS = 46080
LAMBDA_INIT = {1: 0.8 - 0.6 * math.exp(-0.3 * 1), 3: 0.8 - 0.6 * math.exp(-0.3 * 3)}


def build_fused(nlayers=4, debug_out=False):
    P = Prog()
    nc = P.nc
    IN = lambda n, s, d=F32: P.dram(n, s, d, "ExternalInput")
    x_in = IN("x", [T, D])
    cT = IN("cT", [128, 16, 1])
    ada_wq = IN("ada_wq", [4, D, 3072])
    ada_bq = IN("ada_bq", [1, 12288])
    ngT_d = IN("ngT", [128, 4, 4, 16])
    ngrows = IN("ngrows", [16, D])
    ab_w_in = IN("ab_w_in", [2, D, 3584])
    ab_w_out = IN("ab_w_out", [2, 1024, D])
    sinkb_d = IN("sinkb", [128, 2, 4])
    c_w_in = IN("c_w_in", [2, D, 6144])
    c_w_out = IN("c_w_out", [2, D, D])
    lamb_d = IN("lamb", [128, 2, 4, 128])
    sublnb_d = IN("sublnb", [128, 2, 256])
    lin_d = IN("lin", [128, 2, 2])
    w_up = IN("w_up", [4, D, 2 * DFF])
    w_down = IN("w_down", [4, DFF, D])
    convT_d = IN("convT", [128, 4, 4, NFC])
    Ld = {tp: IN("L" + tp, [128, TYPES[tp][2] - TYPES[tp][3] + 512]) for tp in TYPES}
    Lc_d = IN("Lc", [128, 4992])
    ident_d = IN("ident", [128, 128], BF16)
    identf_d = IN("identf", [128, 128])
    flags_d = IN("flags", [128, 2])
    idxkv_d = IN("idxkv", [128, 43], mybir.dt.int32)
    out = P.dram("out", [T, D], F32, "ExternalOutput")

    mown = P.dram("mown", [12, 1024], F32)
    mall = P.dram("mall", [48, 1024], F32)
    qown = P.dram("qown", [16, 128, T], BF16)
    k_ab_own = P.dram("k_ab_own", [768, 1024], BF16)
    k_ab_all = P.dram("k_ab_all", [3072, 1024], BF16)
    v_ab_own = P.dram("v_ab_own", [1024, 774], BF16)
    v_ab_all = P.dram("v_ab_all", [4096, 774], BF16)
    kv_c_own = P.dram("kv_c_own", [4096, 1024], BF16)
    kv_c_all = P.dram("kv_c_all", [4 * 4096, 1024], BF16)
    kvc_chunks = []
    for i in range(8):
        cb = Buf("kvc_chunk%d" % i, kv_c_all.t, kv_c_all.ap[2048 * i:2048 * (i + 1), :])
        kvc_chunks.append(cb)
    kvo_chunks = [Buf("kvo_chunk%d" % i, kv_c_own.t, kv_c_own.ap[512 * i:512 * (i + 1), :]) for i in range(8)]
    xmid = P.dram("xmid", [T, D], F32)
    xnext = [P.dram("xnext%d" % i, [T, D], F32) for i in range(2)]
    yscr = P.dram("yscr", [T, D], F32)
    hown = P.dram("hown", [2, D], F32)
    hall = P.dram("hall", [8, D], F32)

    def PS(name, shape, dt):
        b = P.sb(name, shape, dt)
        b.persistent = True
        return b
    ident = PS("ident_s", [128, 128], BF16)
    identf = PS("identf_s", [128, 128], F32)
    modTall = PS("modTall", [128, 4, 96], F32)
    ngS = PS("ngS", [128, 4, 4, 16], F32)
    stat = [PS("stat%d" % i, [128, 16], F32) for i in range(4)]
    sm = PS("small", [128, 64], F32)
    GT = PS("GT", [128, 16], F32)
    SHT = PS("SHT", [128, 16], F32)
    flagS = PS("flagS", [128, 2], F32)
    idxkv = PS("idxkv_s", [128, 43], mybir.dt.int32)
    banks = [P.ps("bank%d" % i, [128, 512], F32) for i in range(8)]

    def bfview(b):
        v = Buf(b.name + "v", b.t, b.ap.bitcast(BF16).rearrange("p (a b) -> p a b", b=128))
        v.w = b.w
        v.r = b.r
        return v
    bankv = [bfview(b) for b in banks]
    A = Arena(P, "arena", ARENA_WORDS)

    def stage_begin():
        P.barrier()
        P.release_stage_sems()
        A.reset(0)
        for b in banks:
            b.w.clear()
            b.r.clear()

    for (s, d) in ((ident, ident_d), (identf, identf_d), (ngS, ngT_d), (flagS, flags_d), (idxkv, idxkv_d)):
        P.dma("sp", s.ap, d.ap, [d], [s], s)
    for st_ in stat:
        P.memset("dve", st_.ap, 0.0, [st_])
    P.memset("dve", sm.ap, 0.0, [sm])
    regs = {}

    def init_regs(e):
        ins = None
        for nm, val in (("k", 3071), ("v", 4095), ("h", 7)):
            regs[nm] = e.alloc_register("bnd_" + nm)
            ins = e.reg_mov(regs[nm], val)
        return ins
    P.ops["pool"].append(("op", init_regs, None, 0))

    cS = A.take("cS", [128, 16, 1], F32)
    bS = A.take("bS", [1, 12288], F32)
    oS = A.take("oS", [1, 12288], F32)
    NWM = 4
    wbm = [A.take("wbm", [128, 16, 512], BF16) for _ in range(NWM)]
    cSb = A.take("cSb", [128, 16, 2], BF16)
    P.dma("sp", cS.ap, cT.ap, [cT], [cS], cS)
    P.dma("sp", bS[0:1, :], ada_bq.ap, [ada_bq], [bS], bS)
    P.act(cSb[:, :, 0:1], cS.ap, AF.Silu, [cS], [cSb])
    blks = [(l, nb) for l in range(4) for nb in range(6)]

    def load_m(i):
        l, nb = blks[i]
        P.dma("pool", wbm[i % NWM].ap, ada_wq.ap[l].rearrange("(c p) n -> p c n", p=128)[:, :, nb * 512:(nb + 1) * 512],
              [ada_wq], [wbm[i % NWM]], wbm[i % NWM])
    for i_ in range(NWM - 1):
        load_m(i_)
    for i, (l, nb) in enumerate(blks):
        pp = banks[i % 2]
        for c in range(16):
            P.mm(pp[0:1, :], cSb[:, c, 0:1], wbm[i % NWM][:, c, :], c == 0, c == 15, [cSb, wbm[i % NWM]], [pp], signal=(c == 15))
        o = l * 3072 + nb * 512
        P.tt("dve", oS[0:1, o:o + 512], pp[0:1, :], bS[0:1, o:o + 512], ALU.add, [pp, bS], [oS])
        if i + NWM - 1 < len(blks):
            load_m(i + NWM - 1)
    P.dma("sp", mown.ap.rearrange("a b -> (a b)").rearrange("(o n) -> o n", o=1), oS[0:1, :], [oS], [mown], oS)
    P.op_sem("pool", lambda e: e.collective_compute("AllGather", ALU.bypass, replica_groups=[[0, 1, 2, 3], [4, 5, 6, 7]],
                                                   ins=[mown.ap], outs=[mall.ap]), [mown], [mall], "cc", 1)
    mflat = mall.ap.rearrange("a b -> (a b)")
    mrow = A.take("mrow", [96, 128], F32)
    for l in range(4):
        for r in range(4):
            o = (r * 4 + l) * 3072
            P.dma("sp", mrow[24 * r:24 * r + 24, :], mflat[o:o + 3072].rearrange("(a b) -> a b", b=128), [mall], [mrow], mrow)
        P.tr(banks[2][:, 0:96], mrow[0:96, :], identf[0:96, 0:96], [mrow, identf], [banks[2]])
        P.copy("dve", modTall[:, l, :], banks[2][:, 0:96], [banks[2]], [modTall])

    def grow_ap(l, which):
        r = 1 if which == 0 else 3
        o = (r * 4 + l) * 3072 + 1024
        return mflat[o:o + 2048].rearrange("(o n) -> o n", o=1).partition_broadcast(128)

    def stage_LA(kind, layer, xcur):
        j = layer // 2
        stage_begin()
        modT = Buf("modTv", modTall.t, modTall[:, layer, :].rearrange("p (a b) -> p a b", b=16))
        modT.w, modT.r = modTall.w, modTall.r
        ngv = Buf("ngv", ngS.t, ngS[:, layer, :, :])
        ngv.w, ngv.r = ngS.w, ngS.r
        emit_mod_prep(P, modT, ngv, 0, 1, 0, GT, SHT)
        hT = A.take("hT", [128, DC, T], BF16)
        xb = [A.take("xb", [128, D], F32) for _ in range(2)]
        xnb = [A.take("xnb", [128, D], BF16) for _ in range(2)]
        junk = A.take("junk", [128, D], BF16)
        NWB = 7
        wb = [A.take("wb", [128, DC, 512], BF16) for _ in range(NWB)]
        stg_f = [A.take("stgf", [128, T], BF16) for _ in range(2)]
        if kind == "ab":
            stg_t = [A.take("stgt", [128, 4, 129], BF16) for _ in range(3)]
            for s in stg_t:
                P.memset("pool", s[:, :, 128:129], 1.0, [s])
            w_in = ab_w_in
            W = 3584
            kvo = k_ab_own
            vvo = v_ab_own
            kview = kvo.ap.rearrange("(h p) t -> h p t", p=128)
            vview = vvo.ap.rearrange("t (h n) -> t h n", n=129)
            blocks = [
                (0, [(0, ("q", 0)), (128, ("q", 1)), (256, ("q", 2)), (384, ("q", 3))], []),
                (512, [(0, ("k", 0)), (128, ("k", 1))], [(256, 2, 0)]),
                (1024, [(i * 128, ("q", 4 + i)) for i in range(4)], []),
                (1536, [(i * 128, ("q", 8 + i)) for i in range(4)], []),
                (2048, [(i * 128, ("q", 12 + i)) for i in range(4)], []),
                (2560, [(i * 128, ("k", 2 + i)) for i in range(4)], []),
                (3072, [], [(0, 4, 2)]),
            ]
        else:
            stg_t = [A.take("stgt", [128, 512], BF16) for _ in range(3)]
            w_in = c_w_in
            W = 6144
            kvo = kv_c_own
            vvo = kv_c_own
            kview = kvo.ap[0:2048, :].rearrange("(h p) t -> h p t", p=128)
            vviews = [kvo.ap[2048 + 512 * i_:2048 + 512 * (i_ + 1), :].rearrange("q (a c) -> (q a) c", a=2) for i_ in range(4)]
            vview = None
            blocks = []
            for i in range(4):
                blocks.append((i * 512, [(jj * 128, ("q", i * 4 + jj)) for jj in range(4)], []))
            for i in range(4):
                blocks.append((2048 + i * 512, [(jj * 128, ("k", i * 4 + jj)) for jj in range(4)], []))
            for i in range(4):
                blocks.append((4096 + i * 512, [], [(0, 512, i * 512)]))
        w3 = w_in.ap[j].rearrange("(c p) n -> p c n", p=128)

        def load_w(bi):
            c0 = blocks[bi][0]
            b = wb[bi % NWB]
            P.dma("pool", b.ap, w3[:, :, c0:c0 + 512], [w_in], [b], b)
        GRP = [[0, 1, 2, 3], [4, 5, 6, 7]]

        def gather(src_, dst_, r0, nr):
            wb_ = kvc_chunks[r0 // 512] if dst_ is kv_c_all else dst_
            rb_ = kvo_chunks[r0 // 512] if src_ is kv_c_own else src_
            P.op_sem("pool", lambda e: e.collective_compute(
                "AllGather", ALU.bypass, replica_groups=GRP,
                ins=[src_.ap[r0:r0 + nr, :]], outs=[dst_.ap[4 * r0:4 * r0 + 4 * nr, :]]), [rb_], [wb_], "cc", 1)
        if kind == "ab":
            order = [1, 5, 6, 0, 2, 3, 4]
            after = {5: [(k_ab_own, k_ab_all, 0, 384), (k_ab_own, k_ab_all, 384, 384)],
                     6: [(v_ab_own, v_ab_all, 0, 512), (v_ab_own, v_ab_all, 512, 512)]}
        else:
            order = [8, 9, 10, 11, 4, 5, 6, 7, 0, 1, 2, 3]
            after = {}
            for i in range(4):
                after[8 + i] = [(kv_c_own, kv_c_all, 512 * (4 + i), 512)]
                after[4 + i] = [(kv_c_own, kv_c_all, 512 * i, 512)]
        blocks_o = [blocks[i] for i in order]
        after_o = [after.get(i, []) for i in order]
        blocks = blocks_o
        for i_ in range(min(NWB - 1, len(blocks))):
            load_w(i_)
        pst = [bankv[6], bankv[7]]
        for t in range(NT):
            emit_norm_T(P, xcur, t * 128, 128, hT, t * 128, GT, SHT, ident, xb, xnb, junk, stat, pst, t)
        psm = banks[0:4]
        pi = fi = ti = 0
        for bi, (c0, funits, tranges) in enumerate(blocks):
            if bi + NWB - 1 < len(blocks):
                load_w(bi + NWB - 1)
            b = wb[bi % NWB]
            for (off, (which, idx)) in funits:
                sg = stg_f[fi % 2]
                fi += 1
                for half in range(2):
                    pp = psm[pi % 4]
                    pi += 1
                    for c in range(DC):
                        P.mm(pp.ap, b[:, c, off:off + 128], hT[:, c, half * 512:(half + 1) * 512],
                             c == 0, c == DC - 1, [b, hT], [pp], signal=(c == DC - 1))
                    dst = sg[:, half * 512:(half + 1) * 512]
                    sc = SCALE if which == "q" else 1.0
                    if half == 0:
                        P.act(dst, pp.ap, AF.Copy, [pp], [sg], scale=sc)
                    else:
                        P.ts("dve", dst, pp.ap, sc, None, ALU.mult, None, [pp], [sg])
                if which == "q":
                    P.dma("sp", qown.ap[idx], sg.ap, [sg], [qown], sg)
                else:
                    P.dma("sp", kview[idx], sg.ap, [sg], [kvo if kind == "ab" else kvo_chunks[idx // 4]], sg)
            for tr_ in tranges:
                for t in range(NT):
                    pp = psm[pi % 4]
                    pi += 1
                    sg = stg_t[ti % 3]
                    ti += 1
                    if kind == "ab":
                        off, nh, h0 = tr_
                        wdt = nh * 128
                    else:
                        off, wdt, dcol = tr_
                    for c in range(DC):
                        P.mm(pp[:, 0:wdt], hT[:, c, t * 128:(t + 1) * 128], b[:, c, off:off + wdt],
                             c == 0, c == DC - 1, [b, hT], [pp], signal=(c == DC - 1))
                    if kind == "ab":
                        src = pp[:, 0:wdt].rearrange("p (h n) -> p h n", n=128)
                        dsts = sg[:, 0:nh, 0:128]
                    else:
                        src = pp[:, 0:wdt]
                        dsts = sg[:, 0:wdt]
                    if t % 2 == 0:
                        P.act(dsts, src, AF.Copy, [pp], [sg])
                    else:
                        P.copy("dve", dsts, src, [pp], [sg])
                    if kind == "ab":
                        P.dma("sp", vview[t * 128:(t + 1) * 128, h0:h0 + nh, :], sg[:, 0:nh, :], [sg], [vvo], sg)
                    else:
                        P.dma("sp", vviews[dcol // 512][t * 128:(t + 1) * 128, :], sg[:, 0:wdt], [sg], [kvo_chunks[4 + dcol // 512]], sg)
            for g_ in after_o[bi]:
                gather(*g_)

    LAG = 3
    NSB = 4

    def attn_blocks(jobs, vw, obank, sbank, tmp, PT, cnt):
        n = len(jobs)
        for i in range(n + LAG):
            if i < n:
                kap, qap, vap, lap, slope, Rl = jobs[i]
                S = sbank[(cnt + i) % NSB]
                tm = tmp[(cnt + i) % NSB]
                pt = PT[(cnt + i) % NSB]
                P.mm(S.ap, kap, qap, True, True, Rl, [S])
                P.stt("dve", tm.ap, lap, -slope, S.ap, ALU.mult, ALU.add, [S] + Rl, [tm])
                P.act(pt.ap, tm.ap, AF.Exp, [tm], [pt])
            if i >= LAG:
                ii = i - LAG
                kap, qap, vap, lap, slope, Rl = jobs[ii]
                pt = PT[(cnt + ii) % NSB]
                for qs in range(4):
                    P.mm(obank[qs][:, 0:vw], pt[:, qs * 128:(qs + 1) * 128], vap,
                         ii == 0, ii == n - 1, [pt] + Rl, [obank[qs]], signal=(qs == 3))
        return cnt + n

    def stage_LB(kind, layer, xcur):
        j = layer // 2
        ab = kind == "ab"
        KC = 8 if ab else 16
        VW = 129 if ab else 257
        stage_begin()
        oT = A.take("oT", [128, KC, T], BF16)
        tmp = [A.take("tmp", [128, 512], F32) for _ in range(4)]
        PT = [A.take("PT", [128, 512], BF16) for _ in range(4)]
        on = [A.take("on", [128, 4, VW - 1], BF16) for _ in range(2)]
        mark = A.off
        sbank = [banks[0], banks[1], banks[7], banks[6]]
        obank = banks[2:6]
        tbv = bankv[6]
        cnt = 0
        if ab:
            qa = A.take("qall", [128, 16, T], BF16)
            ka = A.take("kall", [128, 6, 3072], BF16)
            va = A.take("vall", [128, 24, 774], BF16)
            Ls = {tp: A.take("Ls" + tp, [128, TYPES[tp][2] - TYPES[tp][3] + 512], F32) for tp in TYPES}
            P.dma("sp", qa.ap, qown.ap.rearrange("h p t -> p h t"), [qown], [qa], qa)
            for tp in TYPES:
                P.dma("sp", Ls[tp].ap, Ld[tp].ap, [Ld[tp]], [Ls[tp]], Ls[tp])
            kviews = [[Buf("kv_%d_%d" % (hd, s_), ka.t, ka[:, hd, s_ * 1024:(s_ + 1) * 1024]) for s_ in range(3)] for hd in range(6)]
            vviews_ = [Buf("vv_%d" % wk, va.t, va[:, wk, :]) for wk in range(24)]
            allk = [b_ for row_ in kviews for b_ in row_]
            P.memset("dve", ka.ap.rearrange("p a b -> p (a b)"), 0.0, allk)
            P.memset("dve", va.ap.rearrange("p a b -> p (a b)"), 0.0, vviews_)

            def gk(hd, s_):
                col = s_ * 6 + hd
                kb_ = kviews[hd][s_]
                P.op_sem("pool", lambda e: e.indirect_dma_start(
                    out=kb_.ap, out_offset=None, in_=k_ab_all.ap[:, :],
                    in_offset=bass.IndirectOffsetOnAxis(ap=idxkv[:, col:col + 1], axis=0),
                    bounds_check=regs["k"], oob_is_err=False), [k_ab_all, idxkv], [kb_], P.semkey_for(kb_), 16)

            def gv(wk):
                col = 18 + wk
                vb_ = vviews_[wk]
                P.op_sem("pool", lambda e: e.indirect_dma_start(
                    out=vb_.ap, out_offset=None, in_=v_ab_all.ap[:, :],
                    in_offset=bass.IndirectOffsetOnAxis(ap=idxkv[:, col:col + 1], axis=0),
                    bounds_check=regs["v"], oob_is_err=False), [v_ab_all, idxkv], [vb_], P.semkey_for(vb_), 16)
            for s_ in range(3):
                gk(0, s_)
            for wk in range(7, 17):
                gv(wk)
            for hd in range(1, 6):
                for s_ in range(3):
                    gk(hd, s_)
            for wk in list(range(0, 7)) + list(range(17, 24)):
                gv(wk)
            P.dma("sp", sm[:, 0:4], sinkb_d.ap[:, j, :], [sinkb_d], [sm], sm)
            P.act(sm[:, 4:8], sm[:, 0:4], AF.Exp, [sm], [sm])
            outs = []
            for u in range(4):
                outs.append((u, [(u, "A", SLOPES16[u])], u // 2, u))
            for i in range(4):
                outs.append((4 + i, [(4 + i, "g0", SLOPES16[4 + i]), (8 + i, "g1", SLOPES16[8 + i]),
                                     (12 + i, "g2", SLOPES16[12 + i])], 2 + i, None))
            fin = 0
            for (ou, qlist, kv, sink) in outs:
                for qb in range(2):
                    q0 = qb * 512
                    jobs = []
                    for (qu, tp, slope) in qlist:
                        R_, dil, dmax, dmin = TYPES[tp]
                        for dl in range(dmax, dmin - 1, -128):
                            wk = (q0 - dl + 1024) // 128
                            jobs.append((ka[:, kv, wk * 128:(wk + 1) * 128], qa[:, qu, q0:q0 + 512],
                                         va[:, wk, kv * 129:(kv + 1) * 129], Ls[tp][:, dl - dmin:dl - dmin + 512],
                                         slope, [kviews[kv][wk // 8], qa, vviews_[wk], Ls[tp]]))
                    cnt = attn_blocks(jobs, 129, obank, sbank, tmp, PT, cnt)
                    onb = on[fin % 2]
                    fin += 1
                    for qs in range(4):
                        dn = sm[:, 8 + qs:9 + qs]
                        if sink is not None:
                            P.tt("dve", dn, obank[qs][:, 128:129], sm[:, 4 + sink:5 + sink], ALU.add, [obank[qs], sm], [sm])
                        else:
                            P.copy("dve", dn, obank[qs][:, 128:129], [obank[qs]], [sm])
                        P.op("dve", lambda e, dn=dn: e.reciprocal(out=dn, in_=dn), [sm], [sm])
                        P.ts("dve", onb[:, qs, :], obank[qs][:, 0:128], dn, None, ALU.mult, None, [obank[qs], sm], [onb])
                    for qs in range(4):
                        P.tr(tbv[:, qs, :], onb[:, qs, :], ident.ap, [onb, ident], [tbv], signal=(qs == 3))
                    P.act(oT[:, ou, q0:q0 + 512], tbv.ap[:, 0:4, :].rearrange("p a b -> p (a b)"), AF.Copy, [tbv], [oT])
            w_out = ab_w_out
        else:
            kh = [A.take("kh", [128, 2, 4096], BF16) for _ in range(2)]
            vh = [A.take("vh", [128, 32, 257], BF16) for _ in range(2)]
            qh = [A.take("qh", [128, 2, T], BF16) for _ in range(2)]
            Lc = A.take("Lc_s", [128, 4992], F32)
            o1 = A.take("o1", [128, 4, 257], F32)
            of = A.take("of", [128, 256], F32)
            jk = A.take("jk", [128, 256], BF16)
            lam = A.take("lam_s", [128, 4, 128], F32)
            SG = A.take("SG", [128, 256], F32)
            P.dma("sp", Lc.ap, Lc_d.ap, [Lc_d], [Lc], Lc)
            P.dma("sp", lam.ap, lamb_d.ap[:, j, :, :], [lamb_d], [lam], lam)
            P.dma("sp", SG.ap, sublnb_d.ap[:, j, :], [sublnb_d], [SG], SG)
            P.dma("sp", sm[:, 0:2], lin_d.ap[:, j, :], [lin_d], [sm], sm)
            for jj in range(2):
                P.tt("dve", lam[:, 2 * jj, :], lam[:, 2 * jj, :], lam[:, 2 * jj + 1, :], ALU.mult, [lam], [lam])
                P.op("dve", lambda e, jj=jj: e.tensor_reduce(out=sm[:, 4 + jj:5 + jj], in_=lam[:, 2 * jj, :], axis=AX.X, op=ALU.add),
                     [lam], [sm])
            P.act(sm[:, 4:6], sm[:, 4:6], AF.Exp, [sm], [sm])
            P.tt("dve", sm[:, 2:3], sm[:, 4:5], sm[:, 5:6], ALU.subtract, [sm], [sm])
            P.tt("dve", sm[:, 2:3], sm[:, 2:3], sm[:, 0:1], ALU.add, [sm], [sm])
            P.ts("dve", sm[:, 3:4], sm[:, 2:3], -1.0, None, ALU.mult, None, [sm], [sm])
            P.ts("dve", SG.ap, SG.ap, sm[:, 1:2], None, ALU.mult, None, [SG, sm], [SG])
            kall5 = kv_c_all.ap.rearrange("(c r x) t -> c r x t", c=8, r=4)
            for i in range(2):
                P.memset("pool", vh[i][:, :, 256:257], 1.0, [vh[i]])

            def load_head(h):
                kb, vb, qb_ = kh[h % 2], vh[h % 2], qh[h % 2]
                ko = ((2 * h) % 4) * 128
                for r in range(4):
                    P.dma("sp", kb[:, :, r * 1024:(r + 1) * 1024],
                          kall5[h // 2, r, ko:ko + 256, :].rearrange("(j p) t -> p j t", p=128), [kvc_chunks[h // 2]], [kb], kb)
                    vsrc = kall5[4 + h // 2, r].rearrange("q (a c) -> (q a) c", a=2).rearrange("(k p) c -> p k c", p=128)
                    vo = (h % 2) * 256
                    P.dma("sp", vb[:, r * 8:(r + 1) * 8, 0:256], vsrc[:, :, vo:vo + 256], [kvc_chunks[4 + h // 2]], [vb], vb)
                P.dma("sp", qb_.ap, qown.ap[2 * h:2 * h + 2].rearrange("j p t -> p j t"), [qown], [qb_], qb_)
            load_head(0)
            fin = 0
            for h in range(8):
                if h + 1 < 8:
                    load_head(h + 1)
                khb, vhb, qhb = kh[h % 2], vh[h % 2], qh[h % 2]
                for qb in range(2):
                    q0 = qb * 512
                    for jj in range(2):
                        jobs = []
                        for kt in range(32):
                            c0 = q0 - kt * 128 + 3968
                            jobs.append((khb[:, jj, kt * 128:(kt + 1) * 128], qhb[:, jj, q0:q0 + 512], vhb[:, kt, :],
                                         Lc[:, c0:c0 + 512], SLOPES8[h], [khb, qhb, vhb, Lc]))
                        cnt = attn_blocks(jobs, 257, obank, sbank, tmp, PT, cnt)
                        if jj == 0:
                            for qs in range(4):
                                P.act(o1[:, qs, :], obank[qs][:, 0:257], AF.Copy, [obank[qs]], [o1])
                    onb = on[fin % 2]
                    fin += 1
                    for qs in range(4):
                        r1 = sm[:, 8 + 4 * qs:9 + 4 * qs]
                        r2 = sm[:, 9 + 4 * qs:10 + 4 * qs]
                        ss = sm[:, 10 + 4 * qs:11 + 4 * qs]
                        rs = sm[:, 11 + 4 * qs:12 + 4 * qs]
                        P.op("dve", lambda e, r1=r1, qs=qs: e.reciprocal(out=r1, in_=o1[:, qs, 256:257]), [o1], [sm])
                        P.op("dve", lambda e, r2=r2, qs=qs: e.reciprocal(out=r2, in_=obank[qs][:, 256:257]), [obank[qs]], [sm])
                        P.tt("dve", r2, r2, sm[:, 3:4], ALU.mult, [sm], [sm])
                        P.ts("dve", of.ap, o1[:, qs, 0:256], r1, None, ALU.mult, None, [o1, sm], [of])
                        P.stt("dve", of.ap, obank[qs][:, 0:256], r2, of.ap, ALU.mult, ALU.add, [obank[qs], sm, of], [of])
                        P.memset("dve", ss, 0.0, [sm])
                        P.act(jk.ap, of.ap, AF.Square, [of], [jk, sm], accum_out=ss)
                        P.ts("dve", rs, ss, 1.0 / 256, EPS, ALU.mult, ALU.add, [sm], [sm])
                        P.act(rs, rs, AF.Sqrt, [sm], [sm])
                        P.op("dve", lambda e, rs=rs: e.reciprocal(out=rs, in_=rs), [sm], [sm])
                        P.stt("dve", onb[:, qs, :], of.ap, rs, SG.ap, ALU.mult, ALU.mult, [of, sm, SG], [onb])
                    for qs in range(4):
                        for i in range(2):
                            P.tr(tbv[:, qs * 2 + i, :], onb[:, qs, i * 128:(i + 1) * 128], ident.ap, [onb, ident], [tbv],
                                 signal=(qs == 3 and i == 1))
                    tv = tbv.ap.rearrange("p (a i) b -> p i a b", i=2)
                    for i in range(2):
                        P.act(oT[:, 2 * h + i, q0:q0 + 512].rearrange("p (a b) -> p a b", b=128), tv[:, i, :, :], AF.Copy, [tbv], [oT])
            w_out = c_w_out

        P.barrier()
        A.reset(mark)
        for b in banks:
            b.w.clear()
            b.r.clear()
        wo = [A.take("wo", [128, KC, 512], BF16) for _ in range(4)]
        ggb_ = A.take("GGb", [128, D], F32)
        xt2 = [A.take("xt2", [128, D], F32) for _ in range(2)]
        yo = [A.take("yo", [128, D], F32) for _ in range(2)]
        junk2 = A.take("junk2", [128, 512], BF16)
        wo3 = w_out.ap[j].rearrange("(c p) n -> p c n", p=128)
        for nb in range(4):
            P.dma("pool", wo[nb].ap, wo3[:, :, nb * 512:(nb + 1) * 512], [w_out], [wo[nb]], wo[nb])
        P.dma("sp", ggb_.ap, ngrows.ap[layer * 4 + 1:layer * 4 + 2, :].partition_broadcast(128), [ngrows], [ggb_], ggb_)
        P.dma("sp", yo[0].ap, grow_ap(layer, 0), [mall], [yo[0]], yo[0])
        P.tt("pool", ggb_.ap, ggb_.ap, yo[0].ap, ALU.mult, [ggb_, yo[0]], [ggb_])
        for t in range(NT):
            bs = banks[0:4] if t % 2 == 0 else banks[4:8]
            xx = xt2[t % 2]
            yy = yo[t % 2]
            P.dma("sp", xx.ap, xcur.ap[t * 128:(t + 1) * 128, :], [xcur], [xx], xx)
            for nb in range(4):
                for c in range(KC):
                    P.mm(bs[nb].ap, oT[:, c, t * 128:(t + 1) * 128], wo[nb][:, c, :], c == 0, c == KC - 1, [oT, wo[nb]], [bs[nb]],
                         signal=(c == KC - 1))
            st = stat[t % 4]
            so = 8
            for nb in range(4):
                P.act(junk2.ap, bs[nb].ap, AF.Square, [bs[nb]], [junk2, st], accum_out=st[:, so + nb:so + nb + 1])
            ss = st[:, so + 4:so + 5]
            rs = st[:, so + 5:so + 6]
            P.op("dve", lambda e, ss=ss, st=st: e.tensor_reduce(out=ss, in_=st[:, 8:12], axis=AX.X, op=ALU.add), [st], [st])
            P.ts("dve", rs, ss, 1.0 / D, EPS, ALU.mult, ALU.add, [st], [st])
            P.act(rs, rs, AF.Sqrt, [st], [st])
            P.op("dve", lambda e, rs=rs: e.reciprocal(out=rs, in_=rs), [st], [st])
            for nb in range(4):
                P.stt("dve", yy[:, nb * 512:(nb + 1) * 512], bs[nb].ap, rs, ggb_[:, nb * 512:(nb + 1) * 512], ALU.mult, ALU.mult,
                      [bs[nb], st, ggb_], [yy])
            P.tt("pool", yy.ap, yy.ap, xx.ap, ALU.add, [yy, xx], [yy])
            P.dma("pool", xmid.ap[t * 128:(t + 1) * 128, :], yy.ap, [yy], [xmid], yy)
            if t == 0:
                P.dma("pool", hown.ap[0:1, :], yy[0:1, :], [yy], [hown], yy)
            if t == NT - 1:
                P.dma("pool", hown.ap[1:2, :], yy[127:128, :], [yy], [hown], yy)
        P.op_sem("pool", lambda e: e.collective_compute("AllGather", ALU.bypass, replica_groups=[[0, 1, 2, 3], [4, 5, 6, 7]],
                                                       ins=[hown.ap], outs=[hall.ap]), [hown], [hall], "cc", 1)

    def stage_LC(layer, xdst):
        stage_begin()
        modT = Buf("modTv", modTall.t, modTall[:, layer, :].rearrange("p (a b) -> p a b", b=16))
        modT.w, modT.r = modTall.w, modTall.r
        ngv = Buf("ngv", ngS.t, ngS[:, layer, :, :])
        ngv.w, ngv.r = ngS.w, ngS.r
        emit_mod_prep(P, modT, ngv, 3, 4, 2, GT, SHT)
        aT = A.take("aT", [128, NFC, T], BF16)
        convS = A.take("convS", [128, 4, NFC], F32)
        base = A.off
        P.dma("sp", convS.ap, convT_d.ap[:, layer, :, :], [convT_d], [convS], convS)
        h2T = A.take("h2T", [128, DC, T + 2], BF16)
        wg = [A.take("wg", [128, DC, 256], BF16) for _ in range(2)]
        wu = [A.take("wu", [128, DC, 256], BF16) for _ in range(2)]
        mark = A.off
        xb = [A.take("xb", [128, D], F32) for _ in range(2)]
        xnb = [A.take("xnb", [128, D], BF16) for _ in range(2)]
        junk = A.take("junk", [128, D], BF16)
        w3 = w_up.ap[layer].rearrange("(c p) n -> p c n", p=128)
        NG = (NFC + 1) // 2

        def load_wu(g):
            c0 = g * 256
            wd_ = min(256, DFF - c0)
            P.dma("pool", wg[g % 2][:, :, 0:wd_], w3[:, :, c0:c0 + wd_], [w_up], [wg[g % 2]], wg[g % 2])
            P.dma("pool", wu[g % 2][:, :, 0:wd_], w3[:, :, DFF + c0:DFF + c0 + wd_], [w_up], [wu[g % 2]], wu[g % 2])
        load_wu(0)
        load_wu(1)
        pst = [bankv[6], bankv[7]]
        for t in range(NT):
            emit_norm_T(P, xmid, t * 128, 128, h2T, t * 128, GT, SHT, ident, xb, xnb, junk, stat, pst, t)
        xt = xb[NT % 2]
        P.memset("dve", xt[0:2, :], 0.0, [xt])
        P.op_sem("pool", lambda e: e.indirect_dma_start(
            out=xt[:, :], out_offset=None, in_=hall.ap[:, :],
            in_offset=bass.IndirectOffsetOnAxis(ap=idxkv[:, 42:43], axis=0), bounds_check=regs["h"], oob_is_err=False),
            [hall, idxkv], [xt], P.semkey_for(xt), 16)
        emit_norm_T(P, None, 0, 2, h2T, T, GT, SHT, ident, xb, xnb, junk, stat, pst, NT)
        P.barrier()
        for b in banks:
            b.w.clear()
            b.r.clear()
        A.reset(mark)
        gsb = [A.take("gsb", [128, T + 4], F32) for _ in range(2)]
        acc = [A.take("acc", [128, T], F32) for _ in range(2)]
        usb = [A.take("usb", [128, T], F32) for _ in range(2)]
        sets = [(banks[0], banks[1]), (banks[2], banks[3]), (banks[4], banks[5])]
        si = 0
        ph = banks[6]
        for fc in range(NFC):
            g = fc // 2
            off = (fc % 2) * 128
            if fc % 2 == 0 and g >= 1 and g + 1 < NG:
                load_wu(g + 1)
            wgb, wub = wg[g % 2], wu[g % 2]
            gs, ac, us = gsb[fc % 2], acc[fc % 2], usb[fc % 2]
            hc = (fc % 8) * 2
            sg = sets[si % 3]
            si += 1
            for half in range(2):
                for c in range(DC):
                    P.mm(sg[half].ap, wgb[:, c, off:off + 128], h2T[:, c, half * 512:(half + 1) * 512],
                         c == 0, c == DC - 1, [wgb, h2T], [sg[half]], signal=(c == DC - 1))
            for c in range(DC):
                P.mm(ph[:, hc:hc + 2], wgb[:, c, off:off + 128], h2T[:, c, T:T + 2],
                     c == 0, c == DC - 1, [wgb, h2T], [ph], signal=(c == DC - 1))
            P.act(gs[:, 1:513], sg[0].ap, AF.Copy, [sg[0]], [gs])
            P.act(gs[:, 513:1025], sg[1].ap, AF.Copy, [sg[1]], [gs])
            P.tt("dve", gs[:, 0:1], ph[:, hc:hc + 1], flagS[:, 0:1], ALU.mult, [ph, flagS], [gs])
            P.tt("dve", gs[:, 1025:1026], ph[:, hc + 1:hc + 2], flagS[:, 1:2], ALU.mult, [ph, flagS], [gs])
            su = sets[si % 3]
            si += 1
            for half in range(2):
                for c in range(DC):
                    P.mm(su[half].ap, wub[:, c, off:off + 128], h2T[:, c, half * 512:(half + 1) * 512],
                         c == 0, c == DC - 1, [wub, h2T], [su[half]], signal=(c == DC - 1))
            P.act(us[:, 0:512], su[0].ap, AF.Copy, [su[0]], [us])
            P.act(us[:, 512:1024], su[1].ap, AF.Copy, [su[1]], [us])
            P.ts("dve", ac.ap, gs[:, 0:T], convS[:, 0, fc:fc + 1], None, ALU.mult, None, [gs, convS], [ac])
            P.stt("dve", ac.ap, gs[:, 1:T + 1], convS[:, 1, fc:fc + 1], ac.ap, ALU.mult, ALU.add, [gs, convS, ac], [ac])
            P.stt("dve", ac.ap, gs[:, 2:T + 2], convS[:, 2, fc:fc + 1], ac.ap, ALU.mult, ALU.add, [gs, convS, ac], [ac])
            P.act(ac.ap, ac.ap, AF.Gelu_apprx_tanh, [ac, convS], [ac], bias=convS[:, 3, fc:fc + 1])
            P.tt("pool", aT[:, fc, :], ac.ap, us.ap, ALU.mult, [ac, us], [aT])
        P.barrier()
        for b in banks:
            b.w.clear()
            b.r.clear()
        A.reset(base)
        wd = [A.take("wd", [128, NFC, 256], BF16) for _ in range(2)]
        yst = [A.take("yst", [128, 256], F32) for _ in range(4)]
        ggb_ = A.take("GGb", [128, D], F32)
        yt = [A.take("yt", [128, D], F32) for _ in range(2)]
        xt2 = [A.take("xt2", [128, D], F32) for _ in range(2)]
        junk2 = A.take("junk2", [128, D], BF16)
        wd3 = w_down.ap[layer].rearrange("(c p) n -> p c n", p=128)

        def load_wd(nb):
            P.dma("pool", wd[nb % 2].ap, wd3[:, :, nb * 256:(nb + 1) * 256], [w_down], [wd[nb % 2]], wd[nb % 2])
        load_wd(0)
        load_wd(1)
        P.dma("sp", ggb_.ap, ngrows.ap[layer * 4 + 3:layer * 4 + 4, :].partition_broadcast(128), [ngrows], [ggb_], ggb_)
        P.dma("sp", yt[0].ap, grow_ap(layer, 1), [mall], [yt[0]], yt[0])
        P.tt("pool", ggb_.ap, ggb_.ap, yt[0].ap, ALU.mult, [ggb_, yt[0]], [ggb_])
        bi = 0
        for nb in range(8):
            wdb = wd[nb % 2]
            for t in range(NT):
                pp = banks[bi % 6]
                ys = yst[bi % 4]
                bi += 1
                for fc in range(NFC):
                    P.mm(pp[:, 0:256], aT[:, fc, t * 128:(t + 1) * 128], wdb[:, fc, :],
                         fc == 0, fc == NFC - 1, [aT, wdb], [pp], signal=(fc == NFC - 1))
                if bi % 2 == 0:
                    P.act(ys.ap, pp[:, 0:256], AF.Copy, [pp], [ys])
                else:
                    P.copy("dve", ys.ap, pp[:, 0:256], [pp], [ys])
                P.dma("sp", yscr.ap[t * 128:(t + 1) * 128, nb * 256:(nb + 1) * 256], ys.ap, [ys], [yscr], ys)
            if nb + 2 < 8:
                load_wd(nb + 2)
        emit_resid(P, yscr, xmid, xdst, ggb_, yt, xt2, stat, junk2)

    xcur = x_in
    for layer in range(nlayers):
        kind = "ab" if layer % 2 == 0 else "c"
        stage_LA(kind, layer, xcur)
        stage_LB(kind, layer, xcur)
        if debug_out:
            dm = P.dram("dbg_m%d" % layer, [T, D], F32, "ExternalOutput")
            P.dma("sp", dm.ap, xmid.ap, [xmid], [dm], stat[0])
        xdst = out if layer == nlayers - 1 else xnext[layer % 2]
        stage_LC(layer, xdst)
        if debug_out and layer != nlayers - 1:
            dx = P.dram("dbg_x%d" % layer, [T, D], F32, "ExternalOutput")
            P.dma("sp", dx.ap, xdst.ap, [xdst], [dx], stat[0])
        xcur = xdst
    return P.finish()
import math
from concourse.bass_utils import run_bass_kernel_spmd
_BF = ml_dtypes.bfloat16
_NC = {}


def _fused_inputs(x, c, ada_w, ada_b, norm_g, ab_w_in, ab_w_out, a_sink, c_w_in, c_w_out,
                  c_lambda, c_subln_g, ffn_w_up, ffn_conv_w, ffn_conv_b, ffn_w_down):
    xf = x.reshape(8192, 2048)
    ident = np.eye(128, dtype=_BF)
    identf = np.eye(128, dtype=np.float32)
    ngT = np.ascontiguousarray(norm_g.reshape(4, 4, 16, 128).transpose(3, 0, 1, 2))
    ngrows = np.ascontiguousarray(norm_g.reshape(16, 2048))
    sinkb = np.ascontiguousarray(np.broadcast_to(a_sink[None], (128, 2, 4)))
    lamb = np.ascontiguousarray(np.broadcast_to(c_lambda[None], (128, 2, 4, 128)))
    sublnb = np.ascontiguousarray(np.broadcast_to(c_subln_g[None], (128, 2, 256)))
    lis = [0.8 - 0.6 * math.exp(-0.3 * l) for l in (1, 3)]
    lin = np.ascontiguousarray(np.broadcast_to(np.array([[li, 1.0 - li] for li in lis], np.float32)[None], (128, 2, 2)))
    cw = np.concatenate([ffn_conv_w, ffn_conv_b[:, None, :]], 1)
    convT = np.ascontiguousarray(cw.reshape(4, 4, 43, 128).transpose(3, 0, 1, 2))
    lt = {tp: ltile_host(tp) for tp in TYPES}
    adaq = [np.ascontiguousarray(ada_w[:, :, q * 3072:(q + 1) * 3072]) for q in range(4)]
    maps = []
    p = np.arange(128)
    for core in range(8):
        b, qr = core // 4, core % 4
        idx = np.full((128, 43), BIGIDX, np.int32)
        for s in range(3):
            r = qr - 1 + s
            if 0 <= r <= 3:
                for hd in range(6):
                    idx[:, s * 6 + hd] = (hd // 3) * 1536 + r * 384 + (hd % 3) * 128 + p
        for wk in range(24):
            tok = qr * 1024 - 1024 + wk * 128 + p
            ok = (tok >= 0) & (tok < 4096)
            tl = tok % 1024
            row = (tl // 512) * 2048 + (tok // 1024) * 512 + tl % 512
            idx[:, 18 + wk] = np.where(ok, row, BIGIDX)
        if qr > 0:
            idx[0, 42] = (qr - 1) * 2 + 1
        if qr < 3:
            idx[1, 42] = (qr + 1) * 2
        fl = np.zeros((128, 2), np.float32)
        fl[:, 0] = 1.0 if qr > 0 else 0.0
        fl[:, 1] = 1.0 if qr < 3 else 0.0
        m = {"x": np.ascontiguousarray(xf[core * 1024:(core + 1) * 1024]),
             "cT": np.ascontiguousarray(c[b].reshape(16, 128).T.reshape(128, 16, 1)),
             "ada_wq": adaq[qr], "ada_bq": np.ascontiguousarray(ada_b[:, qr * 3072:(qr + 1) * 3072].reshape(1, 12288)),
             "ngT": ngT, "ngrows": ngrows, "ab_w_in": ab_w_in, "ab_w_out": ab_w_out, "sinkb": sinkb,
             "c_w_in": c_w_in, "c_w_out": c_w_out, "lamb": lamb, "sublnb": sublnb, "lin": lin,
             "w_up": ffn_w_up, "w_down": ffn_w_down, "convT": convT, "Lc": lc_host(qr),
             "ident": ident, "identf": identf, "flags": fl, "idxkv": idx}
        for tp in TYPES:
            m["L" + tp] = lt[tp]
        maps.append(m)
    return maps


def kernel(x, c, ada_w, ada_b, norm_g, ab_w_in, ab_w_out, a_sink, c_w_in, c_w_out,
           c_lambda, c_subln_g, ffn_w_up, ffn_conv_w, ffn_conv_b, ffn_w_down, _nlayers=4, _debug=None):
    f = lambda a: np.ascontiguousarray(np.asarray(a, dtype=np.float32))
    args = [f(a) for a in (x, c, ada_w, ada_b, norm_g, ab_w_in, ab_w_out, a_sink, c_w_in, c_w_out,
                           c_lambda, c_subln_g, ffn_w_up, ffn_conv_w, ffn_conv_b, ffn_w_down)]
    maps = _fused_inputs(*args)
    key = (_nlayers, _debug is not None)
    if key not in _NC:
        _NC[key] = build_fused(_nlayers, debug_out=_debug is not None)
    res = run_bass_kernel_spmd(_NC[key], maps, core_ids=list(range(8)))
    if _debug is not None:
        _debug.extend(res.results)
    return np.concatenate([r["out"] for r in res.results], 0).reshape(2, 4096, 2048).astype(np.float32)
```
